# Optimizing a Trainium2 kernel written in Bass

```python
import jax, jax.numpy as jnp
from jax import lax
import numpy as np

D_MODEL = 1024
BATCH = 32
SEQ = 2048
DEPTH = 1
DEC_BATCH = 32
DEC_SEQ = 16
PAST_LEN = 4096

CHUNK = 64
WINDOW = 128
WIN_CHUNKS = WINDOW // CHUNK
HEAD_DIM = 64
ATT_HEADS = 8
ATT_KV_HEADS = 2
ATT_GROUP = ATT_HEADS // ATT_KV_HEADS
ATT_SCALE = HEAD_DIM ** -0.5
ROPE_THETA = 10000.0
RWKV_HEADS = 8
RWKV_N = 64
RWKV_W = RWKV_HEADS * RWKV_N
DECAY_LORA = 64
AAA_LORA = 64
GATE_LORA = 160
D_FF = 4 * D_MODEL
ATT_Q = ATT_HEADS * HEAD_DIM
ATT_KV = ATT_KV_HEADS * HEAD_DIM
ATT_COLS = ATT_Q + 2 * ATT_KV
RWKV_COLS = 3 * RWKV_W + DECAY_LORA + AAA_LORA + GATE_LORA
IN_COLS = ATT_COLS + RWKV_COLS
MIX_W = ATT_Q + RWKV_W
RWKV_SPLITS = (RWKV_W, 2 * RWKV_W, 3 * RWKV_W, 3 * RWKV_W + DECAY_LORA, 3 * RWKV_W + DECAY_LORA + AAA_LORA)
NORM_EPS = 1e-6
GN_EPS = 64e-5

kernel_name = 'hymba_swa_sink_rwkv7_stream_step'


def _rms(x, g, eps=NORM_EPS):
    xf = x.astype(jnp.float32)
    y = xf * lax.rsqrt(jnp.mean(xf * xf, axis=-1, keepdims=True) + eps)
    return (y * g.astype(jnp.float32)).astype(x.dtype)


def _rope(x, pos):
    half = HEAD_DIM // 2
    inv = ROPE_THETA ** (-jnp.arange(half, dtype=jnp.float32) / half)
    ang = pos.astype(jnp.float32)[:, None] * inv[None, :]
    cos = jnp.cos(ang)[:, None, :]
    sin = jnp.sin(ang)[:, None, :]
    xf = x.astype(jnp.float32)
    x1, x2 = xf[..., :half], xf[..., half:]
    return jnp.concatenate([x1 * cos - x2 * sin, x2 * cos + x1 * sin], axis=-1).astype(x.dtype)


def _sink_probs(s, sink):
    m = jnp.maximum(jnp.max(s, axis=-1), sink)
    p = jnp.exp(s - m[..., None])
    denom = jnp.sum(p, axis=-1) + jnp.exp(sink - m)
    return p / denom[..., None]


def _swa_prompt(q, k, v, sinks):
    B, S = q.shape[:2]
    nc = S // CHUNK
    span = (WIN_CHUNKS + 1) * CHUNK
    qb = q.reshape(B, nc, CHUNK, ATT_KV_HEADS, ATT_GROUP, HEAD_DIM)
    pad = ((0, 0), (WIN_CHUNKS * CHUNK, 0), (0, 0), (0, 0))
    kp = jnp.pad(k, pad).reshape(B, nc + WIN_CHUNKS, CHUNK, ATT_KV_HEADS, HEAD_DIM)
    vp = jnp.pad(v, pad).reshape(B, nc + WIN_CHUNKS, CHUNK, ATT_KV_HEADS, HEAD_DIM)
    kb = jnp.concatenate([kp[:, j:j + nc] for j in range(WIN_CHUNKS + 1)], axis=2)
    vb = jnp.concatenate([vp[:, j:j + nc] for j in range(WIN_CHUNKS + 1)], axis=2)
    s = jnp.einsum('bnqkgd,bnskd->bnkgqs', qb, kb, preferred_element_type=jnp.float32) * ATT_SCALE
    key_chunk = jnp.arange(nc)[:, None] + jnp.arange(span)[None, :] // CHUNK - WIN_CHUNKS
    s = jnp.where((key_chunk >= 0)[None, :, None, None, None, :], s, -jnp.inf)
    sink = sinks.reshape(ATT_KV_HEADS, ATT_GROUP)[:, :, None].astype(jnp.float32)
    pr = _sink_probs(s, sink)
    o = jnp.einsum('bnkgqs,bnskd->bnqkgd', pr.astype(vb.dtype), vb, preferred_element_type=jnp.float32)
    return o.reshape(B, S, ATT_Q).astype(q.dtype)


def _swa_sample(q, k_all, v_all, sinks):
    B, T = q.shape[:2]
    qg = q.reshape(B, T, ATT_KV_HEADS, ATT_GROUP, HEAD_DIM)
    s = jnp.einsum('btkgd,bskd->bkgts', qg, k_all, preferred_element_type=jnp.float32) * ATT_SCALE
    sink = sinks.reshape(ATT_KV_HEADS, ATT_GROUP)[:, :, None].astype(jnp.float32)
    pr = _sink_probs(s, sink)
    o = jnp.einsum('bkgts,bskd->btkgd', pr.astype(v_all.dtype), v_all, preferred_element_type=jnp.float32)
    return o.reshape(B, T, ATT_Q).astype(q.dtype)


def _wkv_scan(r, w, k, v, a, b, s0):
    def step(S, inp):
        r_t, w_t, k_t, v_t, a_t, b_t = inp
        sa = jnp.einsum('bhij,bhj->bhi', S, a_t)
        S = S * w_t[:, :, None, :] + sa[..., None] * b_t[:, :, None, :] + v_t[..., None] * k_t[:, :, None, :]
        y = jnp.einsum('bhij,bhj->bhi', S, r_t)
        return S, y
    xs = tuple(jnp.moveaxis(t, 1, 0) for t in (r, w, k, v, a, b))
    s_last, ys = lax.scan(step, s0.astype(jnp.float32), xs)
    return jnp.moveaxis(ys, 0, 1), s_last


def _rwkv(p_rw, shift_prev, wkv0, shift_mu, decay_w0, decay_w2, iclr_a0, iclr_a2, gate_g2,
          k_k, k_a, r_k, lnx_g, lnx_b):
    B, T = p_rw.shape[:2]
    f32 = jnp.float32
    prev = jnp.concatenate([shift_prev[:, None, :].astype(p_rw.dtype), p_rw[:, :-1]], axis=1)
    xs = p_rw + (prev - p_rw) * shift_mu
    r, k, v, wd, ad, gd = jnp.split(xs, RWKV_SPLITS, axis=-1)
    w = -jax.nn.softplus(-(decay_w0 + jnp.tanh(wd) @ decay_w2)) - 0.5
    a = jax.nn.sigmoid(iclr_a0 + ad @ iclr_a2)
    g = jax.nn.sigmoid(gd) @ gate_g2
    heads = lambda t: t.reshape(B, T, RWKV_HEADS, RWKV_N).astype(f32)
    kk = heads(k * k_k)
    kk = kk / jnp.maximum(jnp.sqrt(jnp.sum(kk * kk, axis=-1, keepdims=True)), 1e-12)
    k = k * (1 + (a - 1) * k_a)
    rh, kh, vh, ah = heads(r), heads(k), heads(v), heads(a)
    decay = jnp.exp(-jnp.exp(heads(w)))
    y, wkv = _wkv_scan(rh, decay, kh, vh, -kk, kk * ah, wkv0)
    mu = jnp.mean(y, axis=-1, keepdims=True)
    var = jnp.mean(jnp.square(y - mu), axis=-1, keepdims=True)
    y = ((y - mu) * lax.rsqrt(var + GN_EPS)).reshape(B, T, RWKV_W) * lnx_g.astype(f32) + lnx_b.astype(f32)
    bonus = jnp.sum(rh * kh * r_k.astype(f32), axis=-1, keepdims=True) * vh
    y = (y + bonus.reshape(B, T, RWKV_W)) * g.astype(f32)
    return y.astype(p_rw.dtype), wkv, p_rw[:, -1]


def _layer(x, pos, k_past, v_past, shift_prev, wkv0, ln1_g, w_in, q_norm_g, k_norm_g, attn_sinks,
           shift_mu, decay_w0, decay_w2, iclr_a0, iclr_a2, gate_g2, k_k, k_a, r_k, lnx_g, lnx_b,
           w_out, ln2_g, w_up, w_down):
    B, T, _ = x.shape
    h = _rms(x, ln1_g)
    p = h @ w_in
    q, k, v, p_rw = jnp.split(p, (ATT_Q, ATT_Q + ATT_KV, ATT_COLS), axis=-1)
    q = _rope(_rms(q.reshape(B, T, ATT_HEADS, HEAD_DIM), q_norm_g), pos)
    k = _rope(_rms(k.reshape(B, T, ATT_KV_HEADS, HEAD_DIM), k_norm_g), pos)
    v = v.reshape(B, T, ATT_KV_HEADS, HEAD_DIM)
    if k_past is None:
        att = _swa_prompt(q, k, v, attn_sinks)
        rows = min(WINDOW, T)
        new_k, new_v = k[:, T - rows:], v[:, T - rows:]
    else:
        k_all = jnp.concatenate([k_past.astype(k.dtype), k], axis=1)
        v_all = jnp.concatenate([v_past.astype(v.dtype), v], axis=1)
        att = _swa_sample(q, k_all, v_all, attn_sinks)
        new_k, new_v = k, v
    rw, wkv, shift_last = _rwkv(p_rw, shift_prev, wkv0, shift_mu, decay_w0, decay_w2, iclr_a0, iclr_a2,
                                gate_g2, k_k, k_a, r_k, lnx_g, lnx_b)
    x = x + jnp.concatenate([att, rw], axis=-1) @ w_out
    u = jax.nn.relu(_rms(x, ln2_g) @ w_up)
    x = x + (u * u) @ w_down
    return x, new_k, new_v, wkv, shift_last


def setup_inputs(seed: int = 0) -> dict:
    key = jax.random.key(seed)
    ks = iter(jax.random.split(key, 32))
    nrm = lambda shape, scale: jax.random.normal(next(ks), shape, jnp.float32) * scale
    L = DEPTH
    kv_rows = min(WINDOW, PAST_LEN)
    return {
        'x_prompt': nrm((BATCH, SEQ, D_MODEL), 1.0),
        'x_sample': nrm((DEC_BATCH, DEC_SEQ, D_MODEL), 1.0),
        'cache_attn_k': nrm((L, DEC_BATCH, kv_rows, ATT_KV_HEADS, HEAD_DIM), 1.0),
        'cache_attn_v': nrm((L, DEC_BATCH, kv_rows, ATT_KV_HEADS, HEAD_DIM), 1.0),
        'state_rwkv_wkv': nrm((L, DEC_BATCH, RWKV_HEADS, RWKV_N, RWKV_N), 0.3),
        'state_rwkv_shift': nrm((L, DEC_BATCH, RWKV_COLS), 1.0),
        'ln1_g': 1.0 + nrm((L, D_MODEL), 0.02),
        'w_in': nrm((L, D_MODEL, IN_COLS), D_MODEL ** -0.5),
        'q_norm_g': 1.0 + nrm((L, HEAD_DIM), 0.02),
        'k_norm_g': 1.0 + nrm((L, HEAD_DIM), 0.02),
        'attn_sinks': nrm((L, ATT_HEADS), 0.5),
        'shift_mu': jax.random.uniform(next(ks), (L, RWKV_COLS), jnp.float32),
        'decay_w0': -1.0 + nrm((L, RWKV_W), 0.5),
        'decay_w2': nrm((L, DECAY_LORA, RWKV_W), 0.5 * DECAY_LORA ** -0.5),
        'iclr_a0': nrm((L, RWKV_W), 0.3),
        'iclr_a2': nrm((L, AAA_LORA, RWKV_W), 0.5 * AAA_LORA ** -0.5),
        'gate_g2': nrm((L, GATE_LORA, RWKV_W), GATE_LORA ** -0.5),
        'k_k': 0.85 + nrm((L, RWKV_W), 0.02),
        'k_a': 1.0 + nrm((L, RWKV_W), 0.02),
        'r_k': nrm((L, RWKV_HEADS, RWKV_N), 0.1),
        'lnx_g': 1.0 + nrm((L, RWKV_W), 0.02),
        'lnx_b': nrm((L, RWKV_W), 0.02),
        'w_out': nrm((L, MIX_W, D_MODEL), MIX_W ** -0.5),
        'ln2_g': 1.0 + nrm((L, D_MODEL), 0.02),
        'w_up': nrm((L, D_MODEL, D_FF), D_MODEL ** -0.5),
        'w_down': nrm((L, D_FF, D_MODEL), D_FF ** -0.5),
    }


def reference(x_prompt, x_sample, cache_attn_k, cache_attn_v, state_rwkv_wkv, state_rwkv_shift,
              ln1_g, w_in, q_norm_g, k_norm_g, attn_sinks, shift_mu, decay_w0, decay_w2, iclr_a0, iclr_a2,
              gate_g2, k_k, k_a, r_k, lnx_g, lnx_b, w_out, ln2_g, w_up, w_down):
    Bp, Tp = x_prompt.shape[:2]
    Bs, Ts = x_sample.shape[:2]
    pos_p = jnp.arange(Tp)
    pos_s = PAST_LEN + jnp.arange(Ts)
    hp, hs = x_prompt, x_sample
    p_k, p_v, p_wkv, p_shift = [], [], [], []
    s_k, s_v, s_wkv, s_shift = [], [], [], []
    for l in range(DEPTH):
        lw = (ln1_g[l], w_in[l], q_norm_g[l], k_norm_g[l], attn_sinks[l], shift_mu[l], decay_w0[l],
              decay_w2[l], iclr_a0[l], iclr_a2[l], gate_g2[l], k_k[l], k_a[l], r_k[l], lnx_g[l], lnx_b[l],
              w_out[l], ln2_g[l], w_up[l], w_down[l])
        zero_shift = jnp.zeros((Bp, RWKV_COLS), hp.dtype)
        zero_wkv = jnp.zeros((Bp, RWKV_HEADS, RWKV_N, RWKV_N), jnp.float32)
        hp, a1, a2, a3, a4 = _layer(hp, pos_p, None, None, zero_shift, zero_wkv, *lw)
        hs, b1, b2, b3, b4 = _layer(hs, pos_s, cache_attn_k[l], cache_attn_v[l], state_rwkv_shift[l],
                                    state_rwkv_wkv[l], *lw)
        p_k.append(a1); p_v.append(a2); p_wkv.append(a3); p_shift.append(a4)
        s_k.append(b1); s_v.append(b2); s_wkv.append(b3); s_shift.append(b4)
    return (hp, hs, jnp.stack(p_k), jnp.stack(p_v), jnp.stack(p_wkv), jnp.stack(p_shift),
            jnp.stack(s_k), jnp.stack(s_v), jnp.stack(s_wkv), jnp.stack(s_shift))
```

```python
import os
import numpy as np
from contextlib import ExitStack
import concourse.bass as bass
import concourse.mybir as mybir
from concourse.bass_utils import run_bass_kernel_spmd

F32 = mybir.dt.float32
BF16 = mybir.dt.bfloat16
ALU = mybir.AluOpType
AF = mybir.ActivationFunctionType

D = 1024
INC = 2592
DFF = 4096
RWC = 1824
PAST = 4096
NCORES = 8


class Res:
    __slots__ = ("name", "lw", "rd", "psum", "last_kind", "last_alloc")

    def __init__(self, name):
        self.name = name
        self.lw = None
        self.rd = {}
        self.psum = False
        self.last_kind = None
        self.last_alloc = None


class Buf:
    def __init__(self, t, name):
        self.t = t
        self.r = Res(name)
        self.cr = [Res(name + "_c%d" % i) for i in range(4)]

    def c(self, ch):
        return self.cr[ch]

    def __getitem__(self, i):
        return self.t[i]


def _res(xs):
    return [getattr(x, "r", x) for x in xs]


class Prog:
    COMPUTE = ("pe", "act", "dve", "pool")
    QUEUES = ("sp", "act", "pool")

    def __init__(self, nc, ndma_sems=16):
        self.nc = nc
        self.ops = {e: [] for e in ("pe", "act", "dve", "pool", "sp")}
        self.count = {e: 0 for e in self.COMPUTE}
        self.seen = {e: {} for e in ("pe", "act", "dve", "pool", "sp")}
        self.ndma_sems = ndma_sems
        self.dma_n = {}
        self.dma_rr = {q: 0 for q in self.QUEUES}
        self.dma_final = []
        self.capture = None

    def record(self, f):
        outer = self.capture
        self.capture = []
        try:
            f()
        finally:
            lst, self.capture = self.capture, outer
        return lst

    def replay_merged(self, main, side):
        cut = len(main)
        for i, it in enumerate(main):
            if it[0] == "mark":
                cut = i
                break
        head = main[:cut]
        n1, n2 = len(head), len(side)
        j = 0
        for i, it in enumerate(head):
            self._replay(it)
            want = ((i + 1) * n2) // max(n1, 1)
            while j < want:
                self._replay(side[j]); j += 1
        while j < n2:
            self._replay(side[j]); j += 1
        for it in main[cut + 1:]:
            if it[0] != "mark":
                self._replay(it)

    def mark(self):
        if self.capture is not None:
            self.capture.append(("mark",))

    def _replay(self, it):
        if it[0] == "mark":
            if self.capture is not None:
                self.capture.append(it)
            return
        if it[0] == "op":
            self.op(it[1], it[2], it[3], it[4])
        else:
            self.dma(it[1], it[2], it[3], it[4], it[5])

    def _deps(self, reads, writes):
        deps = set()
        for r in reads:
            if r.lw is not None:
                deps.add(r.lw)
        for w in writes:
            if w.lw is not None:
                deps.add(w.lw)
            for s, i in w.rd.items():
                deps.add((s, i))
        return deps

    def _emit_waits(self, eng, deps):
        seen = self.seen[eng]
        best = {}
        for s, i in deps:
            if seen.get(s, 0) >= i:
                continue
            if best.get(s, 0) < i:
                best[s] = i
        for s, i in best.items():
            seen[s] = i
            self.ops[eng].append(("wait", s, i))

    def _mark(self, me, reads, writes):
        for r in reads:
            if r.rd.get(me[0], 0) < me[1]:
                r.rd[me[0]] = me[1]
        for w in writes:
            w.lw = me
            w.rd = {}

    def op(self, eng, fn, reads=(), writes=()):
        if self.capture is not None:
            self.capture.append(("op", eng, fn, list(reads), list(writes), False))
            return None
        for x in writes:
            a_ = getattr(x, "alloc", None)
            if a_ is not None:
                if x.r.last_kind == "w" and x.r.last_alloc != a_:
                    raise RuntimeError("PSUM bank %s overwritten before its previous contents were consumed" % x.r.name)
                x.r.last_kind = "w"; x.r.last_alloc = a_
        for x in reads:
            if getattr(x, "alloc", None) is not None:
                x.r.last_kind = "r"
        reads = _res(reads)
        writes = _res(writes)
        writes = writes + [r for r in reads if r.psum and r not in writes]
        deps = self._deps(reads, writes)
        if eng == "pe":
            deps = {d for d in deps if d[0] != "pe"}
        self._emit_waits(eng, deps)
        self.count[eng] += 1
        me = (eng, self.count[eng])
        self.ops[eng].append(("op", fn, me))
        self._mark(me, reads, writes)
        return me

    def dma(self, q, fn, reads=(), writes=(), final=False):
        if self.capture is not None:
            self.capture.append(("dma", q, fn, list(reads), list(writes), final))
            return None
        reads = _res(reads)
        writes = _res(writes)
        slot = (q, self.dma_rr[q] % self.ndma_sems)
        self.dma_rr[q] += 1
        n = self.dma_n.get(slot, 0)
        deps = self._deps(reads, writes)
        if n > 0:
            deps.add((("d",) + slot, n))
        self._emit_waits(q, deps)
        self.dma_n[slot] = n + 1
        me = (("d",) + slot, n + 1)
        self.ops[q].append(("dma", fn, me))
        self._mark(me, reads, writes)
        if final:
            self.dma_final.append(me)
        return me

    def finish(self):
        self._emit_waits("sp", set(self.dma_final))

    def emit(self, stack):
        nc = self.nc
        sems = {}
        for e in self.COMPUTE:
            sems[e] = stack.enter_context(nc.semaphore("s_" + e))
        for slot in self.dma_n:
            sems[("d",) + slot] = stack.enter_context(nc.semaphore("d_%s_%d" % slot))
        block = stack.enter_context(nc.Block())
        engmap = {"pe": "tensor", "act": "scalar", "dve": "vector", "pool": "gpsimd", "sp": "sync"}

        def run(e, eng):
            for it in self.ops[e]:
                if it[0] == "wait":
                    s, i = it[1], it[2]
                    eng.wait_ge(sems[s], i * 16 if isinstance(s, tuple) else i)
                elif it[0] == "op":
                    it[1](eng).then_inc(sems[it[2][0]], 1)
                else:
                    it[1](eng).then_inc(sems[it[2][0]], 16)

        for e, attr in engmap.items():
            if self.ops[e]:
                getattr(block, attr)(lambda eng, e=e: run(e, eng))


def build(NSEQ, SEQ, NS, TS):
    nc = bass.Bass("TRN2", target_bir_lowering=False)
    GT = 256
    ST = 512
    NPOS = SEQ + NS * TS

    def din(name, shape, dt=F32):
        return nc.dram_tensor(name, list(shape), dt, kind="ExternalInput").ap()

    def dout(name, shape):
        return nc.dram_tensor(name, list(shape), F32, kind="ExternalOutput").ap()

    xp = din("xp", [NSEQ * SEQ, D]); xs = din("xs", [NS * TS, D])
    ck = din("ck", [NS, 128, 128]); cv = din("cv", [NS, 128, 128])
    swkv = din("swkv", [NS, 4, 128, 64]); sshift = din("sshift", [NS, 128, 15])
    g1 = din("g1", [128, 8]); g2 = din("g2", [128, 8])
    w_in = din("w_in", [D, INC]); w_out = din("w_out", [D, D])
    w_up = din("w_up", [D, DFF]); w_down = din("w_down", [DFF, D])
    dw2 = din("dw2", [64, 512]); ia2 = din("ia2", [64, 512]); gg2 = din("gg2", [160, 512])
    pp = din("pp", [128, 48]); sinks = din("sinks", [1, 8])
    identd = din("ident", [128, 128]); rpermd = din("rperm", [64, 64]); masksd = din("masks", [128, 3, 128])
    ropecd = din("ropec", [64, NPOS]); ropesd = din("ropes", [64, NPOS])

    yp = dout("yp", [NSEQ * SEQ, D]); ys = dout("ys", [NS * TS, D])
    pk = dout("pk", [NSEQ, 128, 128]); pv = dout("pv", [NSEQ, 128, 128])
    pwkv = dout("pwkv", [NSEQ, 8, 64, 64]); pshift = dout("pshift", [NSEQ, 128, 15])
    sk = dout("sk", [NS, TS, 128]); sv = dout("sv", [NS, TS, 128])
    swkvo = dout("swkvo", [NS, 8, 64, 64]); sshifto = dout("sshifto", [NS, 128, 15])

    NSL = 16
    wups = nc.dram_tensor("wups", [NSL, 128, 8, 256], BF16, kind="Internal").ap()
    wdns = nc.dram_tensor("wdns", [NSL, 128, 2, D], BF16, kind="Internal").ap()

    st = ExitStack()
    P = Prog(nc)
    STOP = int(os.environ.get('KSTOP', '99'))
    SKIP = os.environ.get('KSKIP', '')

    class StopBuild(Exception):
        pass

    def stage(n):
        if STOP <= n:
            raise StopBuild()

    allbufs = []

    def sb(name, shape, dt=F32):
        b = Buf(st.enter_context(nc.sbuf_tensor("s_" + name, list(shape), dt)), name)
        allbufs.append(b)
        return b

    banks = [Buf(st.enter_context(nc.psum_tensor("pa%d" % i, [128, 512], F32)), "pa%d" % i) for i in range(8)]
    for b_ in banks:
        b_.r.psum = True
    rings = {"all": list(range(6)), "main": list(range(5)), "side": [5], "mlp": [6, 7]}
    rr = {"all": 0, "main": 0, "side": 0, "mlp": 0, "cur": "all"}

    class _View:
        def __init__(self, bank, ncol, alloc):
            self.t = bank.t[:, 0:ncol]
            self.r = bank.r
            self.alloc = alloc

        def __getitem__(self, i):
            return self.t[i]

    def PA(ncol=512):
        k = rr["cur"]
        ring = rings[k]
        b = banks[ring[rr[k] % len(ring)]]; rr[k] += 1
        rr["alloc"] = rr.get("alloc", 0) + 1
        return _View(b, ncol, rr["alloc"])

    def PB():
        return PA(256)

    def tt(eng, out, a, b, op, R, W):
        P.op(eng, lambda e: e.tensor_tensor(out=out, in0=a, in1=b, op=op), R, W)

    def ts(eng, out, a, s1, op0, R, W, s2=None, op1=None):
        if op1 is None:
            P.op(eng, lambda e: e.tensor_scalar(out=out, in0=a, scalar1=s1, scalar2=None, op0=op0), R, W)
        else:
            P.op(eng, lambda e: e.tensor_scalar(out=out, in0=a, scalar1=s1, scalar2=s2, op0=op0, op1=op1), R, W)

    def stt(out, a, s, b, op0, op1, R, W):
        P.op("dve", lambda e: e.scalar_tensor_tensor(out=out, in0=a, scalar=s, in1=b, op0=op0, op1=op1), R, W)

    def act(out, in_, func, R, W, **kw):
        P.op("act", lambda e: e.activation(out=out, in_=in_, func=func, **kw), R, W)

    def cp(eng, out, in_, R, W):
        if eng == "act":
            act(out, in_, AF.Copy, R, W)
        else:
            P.op(eng, lambda e: e.tensor_copy(out=out, in_=in_), R, W)

    def mm(out, lhsT, rhs, R, W, start=True, stop=True):
        P.op("pe", lambda e: e.matmul(out, lhsT=lhsT, rhs=rhs, start=start, stop=stop), R, W)

    def tr(out, in_, ident_ap, R, W):
        P.op("pe", lambda e: e.transpose(out=out, in_=in_, identity=ident_ap), R, W)

    def memset(eng, ap, val, W):
        P.op(eng, lambda e: e.memset(ap, val), (), W)

    def ld(q, out, in_, W, R=()):
        P.dma(q, lambda e: e.dma_start(out=out, in_=in_), R, W)

    def stq(out, in_, R):
        P.dma("sp", lambda e: e.dma_start(out=out, in_=in_), R, (), final=True)

    identf = sb("identf", [128, 128]); identb = sb("identb", [128, 128], BF16)
    rperm = sb("rperm", [64, 64]); masks = sb("masks", [128, 3, 128])
    onesb = sb("onesb", [128, 128], BF16); bones = sb("bones", [128, 128], BF16); bones64 = sb("bones64", [128, 128], BF16)
    bdm = sb("bdm", [128, 2, 64])
    g1bc = sb("g1bc", [128, 8]); g2bc = sb("g2bc", [128, 8])
    ppt = sb("ppt", [128, 48]); omka = sb("omka", [128, 4]); omu = sb("omu", [128, 15]); esink = sb("esink", [64, 8])
    lwb = sb("lwb", [128, 3, 512], BF16)
    winb = sb("winb", [128, 8, INC], BF16); woutb = sb("woutb", [128, 8, D], BF16)
    rmask_p = sb("rmask_p", [128, GT]); rmask_s = sb("rmask_s", [128, 64])
    wsu = [Res("wsu%d" % i) for i in range(16)]; wsd = [Res("wsd%d" % i) for i in range(16)]

    ld("sp", identf[:], identd[:, :], [identf]); ld("sp", rperm[:], rpermd[:, :], [rperm])
    ld("sp", masks[:], masksd[:, :, :], [masks]); ld("sp", ppt[:], pp[:, :], [ppt])
    ld("sp", g1bc[:], g1[:, :], [g1bc]); ld("sp", g2bc[:], g2[:, :], [g2bc])
    ld("sp", esink[:], sinks.broadcast_to([64, 8]), [esink])
    ld("pool", lwb[0:64, 0, :], dw2[:, :], [lwb]); ld("pool", lwb[64:128, 0, :], ia2[:, :], [lwb])
    ld("pool", lwb[:, 1, :], gg2[0:128, :], [lwb]); ld("pool", lwb[0:32, 2, :], gg2[128:160, :], [lwb])
    for hlf in range(2):
        ld("pool", winb[:, :, hlf * 1296:(hlf + 1) * 1296],
           w_in[:, hlf * 1296:(hlf + 1) * 1296].rearrange("(c p) n -> p c n", p=128), [winb])
    ld("pool", woutb[:], w_out.rearrange("(c p) n -> p c n", p=128), [woutb])
    def _convert_ffn_weights():
        for s in range(NSL):
            ld("pool", wups[s], w_up[:, s * 256:(s + 1) * 256].rearrange("(c p) n -> p c n", p=128), [wsu[s]])
            ld("pool", wdns[s], w_down[s * 256:(s + 1) * 256, :].rearrange("(c p) n -> p c n", p=128), [wsd[s]])
    ffn_convert = P.record(_convert_ffn_weights)
    cp("dve", identb[:], identf[:], [identf], [identb])
    memset("dve", onesb[:], 1.0, [onesb])
    memset("pool", bdm[:], 0.0, [bdm]); memset("pool", bdm[0:64, 0, :], 1.0, [bdm]); memset("pool", bdm[64:128, 1, :], 1.0, [bdm])
    cp("dve", bones[:], bdm[:].rearrange("p h t -> p (h t)"), [bdm], [bones])
    ts("dve", bones64[:], bones[:], 1.0 / 64, ALU.mult, [bones], [bones64])
    ts("dve", omka[:], ppt[:, 27:31], -1.0, ALU.mult, [ppt], [omka], 1.0, ALU.add)
    ts("dve", omu[:], ppt[:, 0:15], -1.0, ALU.mult, [ppt], [omu], 1.0, ALU.add)
    act(esink[:], esink[:], AF.Exp, [esink], [esink])
    memset("dve", rmask_p[:], 1.0, [rmask_p])
    memset("dve", rmask_p[:].rearrange("p (c t) -> p c t", t=64)[:, :, 0:1], 0.0, [rmask_p])
    memset("dve", rmask_s[:], 1.0, [rmask_s])
    memset("dve", rmask_s[:].rearrange("p (c t) -> p c t", t=TS)[:, :, 0:1], 0.0, [rmask_s])
    MU, W0, A0, KK, KA, RK, LG, LB, QG, KG = 0, 15, 19, 23, 27, 31, 35, 39, 43, 44

    x1 = sb("x1", [128, 4, D])
    x1r = [Res("x1_%d" % i) for i in range(4)]
    ssq = sb("ssq", [128, 1]); rstd1 = sb("rstd1", [128, 1])
    hb = sb("hb", [128, D], BF16)
    hT = sb("hT", [128, 8, GT], BF16); h2T = sb("h2T", [128, 8, ST], BF16)
    ropec = sb("ropec", [64, GT]); ropes = sb("ropes", [64, GT])
    qT = sb("qT", [64, 8, GT], BF16); kTb = sb("kTb", [64, 2, 128 + GT], BF16); kf = sb("kf", [64, 2, GT])
    Vb = sb("Vb", [128, 3, 128], BF16); vf = sb("vf", [128, 128])
    kTs = sb("kTs", [64, 2, 128], BF16); Vs = sb("Vs", [128, 128], BF16); Vn = sb("Vn", [16, 128], BF16)
    ckf = sb("ckf", [128, 128]); kof = sb("kof", [128, 128])

    PTA = [sb("PTA%d" % i, [128, 4, 128], BF16) for i in range(1)]
    PTB = [sb("PTB%d" % i, [128, 4, 128], BF16) for i in range(1)]
    PSA = sb("PSA", [128, 4, 16], BF16); PSB = sb("PSB", [16, 4, 16], BF16)

    attT = sb("attT", [128, 4, GT], BF16); mixT = sb("mixT", [128, 4, GT], BF16)
    prw = sb("prw", [128, 15, GT]); prwr = [Res("prw%d" % i) for i in range(15)]
    dsh = sb("dsh", [128, 5, GT])
    dshs = []
    for i_ in range(5):
        b_ = Buf(None, "dsh%d" % i_); b_.t = dsh.t[:, i_, :]; dshs.append(b_)
    yTp, msqp, varp, ycp, tmp2 = dshs
    dsh_all = [b_.r for b_ in dshs]
    carryS = sb("carryS", [128, 4, 15]); cmS = sb("cmS", [128, 4, 15]); rawl = sb("rawl", [128, 4, 15])
    lin = sb("lin", [128, 3, GT], BF16)
    sig = sb("sig", [128, GT]); av = sb("av", [128, GT]); gvd = [sb("gv%d" % i, [128, GT]) for i in range(2)]; gv = gvd[0]
    kk = sb("kk", [128, GT]); kkb = sb("kkb", [128, GT], BF16)
    kmd = [sb("km%d" % i, [128, GT]) for i in range(2)]; km = kmd[0]; bb = sb("bb", [128, GT]); tmp = sb("tmp", [128, GT])
    cum = sb("cum", [128, GT]); Wt = sb("Wt", [128, GT]); Wi = sb("Wi", [128, GT])
    rn = cum; Wp = sig
    wcsd = [sb("wcs%d" % i, [128, 4]) for i in range(2)]

    ATRd = [sb("ATR%d" % i, [128, 4, 256], BF16) for i in range(2)]
    BTbd = [sb("BTb%d" % i, [128, 4, 128], BF16) for i in range(2)]; KTbd = [sb("KTb%d" % i, [128, 4, 128], BF16) for i in range(2)]
    F3d = [sb("F3%d" % i, [128, 4, 3, 128], BF16) for i in range(2)]
    T3 = sb("T3", [128, 4, 3, 128], BF16)
    LTR = sb("LTR", [128, 4, 256], BF16)
    AKR = sb("AKR", [128, 4, 256], BF16)
    XX = [sb("XX%d" % i, [128, 4, 256], BF16) for i in range(2)]
    MT = [sb("MT%d" % i, [128, 4, 128], BF16) for i in range(2)]
    Zb = sb("Zb", [128, 128], BF16); Ub = sb("Ub", [128, 128], BF16)
    Tf = [sb("Tf%d" % c, [128, 128]) for c in range(4)]
    Tb = [[sb("Tb%d_%d" % (c, i), [128, 128], BF16) for i in range(2)] for c in range(4)]
    tpar = [0, 0, 0, 0]
    Sld = sb("Sld", [128, 64]); Sbd = sb("Sbd", [128, 128]); Sout = sb("Sout", [128, 128])
    ybfp = sb("ybfp", [128, GT], BF16); y2bf = sb("y2bf", [128, GT], BF16)
    rkr = sb("rkr", [128, GT], BF16)
    sqb2 = sb("sqb2", [64, 2, GT], BF16)
    relu_t = sb("relu_t", [128, 512]); den2 = relu_t; rec = dsh; relu_ms = [sb("relu_m%d" % i, [128, GT]) for i in range(2)]; uT = [sb("uT%d" % i, [128, 2, GT], BF16) for i in range(2)]
    wupr = [sb("wupr%d" % i, [128, 8, 256], BF16) for i in range(2)]
    wdnr = [sb("wdnr%d" % i, [128, 2, D], BF16) for i in range(2)]

    subres = {id(prw): prwr, id(x1): x1r, id(dsh): dsh_all}
    for i_, b_ in enumerate(allbufs[allbufs.index(x1):]):
        if b_ not in wupr and b_ not in wdnr:
            memset("pool" if i_ % 2 == 0 else "dve", b_[:], 0.0, [b_] + list(b_.cr) + list(subres.get(id(b_), [])))

    def rmsnorm_T(xrow_ap, xres, rows, gbc, dstT, col0):
        act(hb[0:rows, :], xrow_ap, AF.Square, [xres], [hb, ssq], accum_out=ssq[0:rows, :])
        act(rstd1[0:rows, :], ssq[0:rows, :], AF.Ln, [ssq], [rstd1], scale=1.0 / D, bias=1e-6)
        act(rstd1[0:rows, :], rstd1[0:rows, :], AF.Exp, [rstd1], [rstd1], scale=-0.5)
        ts("dve", hb[0:rows, :], xrow_ap, rstd1[0:rows, 0:1], ALU.mult, [xres, rstd1], [hb])
        pb = PB()
        pbv = pb.t.bitcast(BF16)
        for half in range(2):
            if half == 1:
                pb = PB(); pbv = pb.t.bitcast(BF16)
            for c4 in range(4):
                c = half * 4 + c4
                tr(pbv[:, c4 * 128:c4 * 128 + rows], hb[0:rows, c * 128:(c + 1) * 128], identb[0:rows, 0:rows], [hb, identb], [pb])
            tt("dve", dstT[:, half * 4:half * 4 + 4, col0:col0 + rows], pbv[:].rearrange("p (c t) -> p c t", t=128)[:, :, 0:rows],
               gbc[:, half * 4:half * 4 + 4].rearrange("p (c o) -> p c o", o=1).broadcast_to([128, 4, rows]), ALU.mult, [pb, gbc], [dstT])

    def attn_core(SA, SB, qap, nq, VA, VB, kA_n, kB_n, regA, regB, PTa, PTb, kvh, ocol):
        N = 4 * nq
        if SA is not None:
            for (p0, p1, q0, q1) in regA:
                act(PTa[p0:p1, :, q0:q1], SA[p0:p1, 0:N].rearrange("p (g q) -> p g q", g=4)[:, :, q0:q1], AF.Exp, [SA], [PTa], scale=0.125)
        for (p0, p1, q0, q1) in regB:
            act(PTb[p0:p1, :, q0:q1], SB[p0:p1, 0:N].rearrange("p (g q) -> p g q", g=4)[:, :, q0:q1], AF.Exp, [SB], [PTb], scale=0.125)
        dn = PA(); ob = PA()
        if SA is not None:
            mm(dn[0:64, 0:N], onesb[0:kA_n, 0:64], PTa[0:kA_n, :, 0:nq], [onesb, PTa], [dn], True, False)
        mm(dn[0:64, 0:N], onesb[0:kB_n, 0:64], PTb[0:kB_n, :, 0:nq], [onesb, PTb], [dn], SA is None, True)
        if SA is not None:
            mm(ob[0:64, 0:N], VA[0], PTa[0:kA_n, :, 0:nq], [VA[1], PTa], [ob], True, False)
        mm(ob[0:64, 0:N], VB[0], PTb[0:kB_n, :, 0:nq], [VB[1], PTb], [ob], SA is None, True)
        tt("dve", den2[0:64, 0:N].rearrange("p (g q) -> p g q", g=4), dn[0:64, 0:N].rearrange("p (g q) -> p g q", g=4),
           esink[:, kvh * 4:kvh * 4 + 4].rearrange("p (g o) -> p g o", o=1).broadcast_to([64, 4, nq]), ALU.add, [dn, esink], [den2])
        recv = rec[:].rearrange("p c t -> p (c t)")
        act(recv[0:64, 0:N], den2[0:64, 0:N], AF.Ln, [den2], dsh_all)
        act(recv[0:64, 0:N], recv[0:64, 0:N], AF.Exp, dsh_all, dsh_all, scale=-1.0)
        o4 = ob[0:64, 0:N].rearrange("p (g q) -> p g q", g=4)
        r4 = recv[0:64, 0:N].rearrange("p (g q) -> p g q", g=4)
        for par in range(2):
            tt("dve", attT[par * 64:par * 64 + 64, kvh * 2:kvh * 2 + 2, ocol:ocol + nq], o4[:, par::2, :], r4[:, par::2, :], ALU.mult, [ob] + dsh_all, [attT])

    def qk_pair(ps, T, gcol, is_k, hp):
        v3 = lambda ap: ap.rearrange("p (j t) -> p j t", j=2)
        p3 = v3(ps[0:64, 0:2 * T])
        rs3 = dsh.t[0:64, 0:2, 0:T]; qn3 = dsh.t[0:64, 2:4, 0:T]
        t13 = relu_t.t[0:64, 0:2 * GT].rearrange("p (j t) -> p j t", j=2)[:, :, 0:T]
        sq3 = sqb2.t[0:64, :, 0:T]
        cos3 = ropec[:, 0:T].rearrange("p (o t) -> p o t", o=1).broadcast_to([64, 2, T])
        sin3 = ropes[:, 0:T].rearrange("p (o t) -> p o t", o=1).broadcast_to([64, 2, T])
        act(sq3, p3, AF.Square, [ps], [sqb2])
        s2 = PA()
        mm(s2[0:64, 0:2 * T], onesb[0:64, 0:64], sq3, [onesb, sqb2], [s2])
        act(rs3, v3(s2[0:64, 0:2 * T]), AF.Ln, [s2], dsh_all, scale=1.0 / 64, bias=1e-6)
        act(rs3, rs3, AF.Exp, dsh_all, dsh_all, scale=-0.5)
        stt(qn3, p3, ppt[0:64, gcol:gcol + 1], rs3, ALU.mult, ALU.mult, [ps, ppt] + dsh_all, dsh_all)
        tt("pool", t13, qn3, cos3, ALU.mult, dsh_all + [ropec], [relu_t])
        tt("dve", rs3[0:32], qn3[32:64], sin3[32:64], ALU.mult, dsh_all + [ropes], dsh_all)
        tt("dve", rs3[32:64], qn3[0:32], sin3[0:32], ALU.mult, dsh_all + [ropes], dsh_all)
        if is_k:
            tt("dve", kf[:, 0:2, 0:T], t13, rs3, ALU.add, [relu_t] + dsh_all, [kf])
            cp("act", kTb[:, 0:2, 128:128 + T], kf[:, 0:2, 0:T], [kf], [kTb])
        else:
            tt("dve", qT[:, 2 * hp:2 * hp + 2, 0:T], t13, rs3, ALU.add, [relu_t] + dsh_all, [qT])

    def stage_A(tiles):
        col = 0
        for (xap, rows, ti) in tiles:
            ld("sp", x1[0:rows, ti, :], xap, [x1r[ti]])
            rmsnorm_T(x1[0:rows, ti, :], x1r[ti], rows, g1bc, hT, col)
            col += rows

    def mixer_group(kind, T, tiles, segs, pos0, h2col0, seq_first, seq_last, seqidx, do_A=True, before_E=None):
        C = 64 if kind == "p" else TS
        nch = T // C
        ld("sp", ropec[:, 0:T], ropecd[:, pos0:pos0 + T], [ropec]); ld("sp", ropes[:, 0:T], ropesd[:, pos0:pos0 + T], [ropes])
        if do_A:
            stage_A(tiles)
        stage(1)
        nseg = len(segs)
        sn0 = segs[0][2]
        for (si, scol, sn, sfirst, slast, sidx) in segs:
            if kind == "s":
                ld("sp", carryS[:, si, :], sshift[sidx], [carryS])
            elif sfirst:
                memset("pool", carryS[:, 0, :], 0.0, [carryS])
        for c in range(15):
            M = 128 if c < 14 else 32
            ps = PA()
            c0 = 768 + c * 128
            for kc in range(8):
                mm(ps[0:M, 0:T], winb[:, kc, c0:c0 + M], hT[:, kc, 0:T], [winb, hT], [ps], kc == 0, kc == 7)
            act(prw[0:M, c, 0:T], ps[0:M, 0:T], AF.Copy, [ps, omu], [prwr[c]], scale=omu[0:M, c:c + 1])
            for (si, scol, sn, sfirst, slast, sidx) in segs:
                stt(prw[0:M, c, scol + 1:scol + sn], ps[0:M, scol:scol + sn - 1], ppt[0:M, MU + c:MU + c + 1], prw[0:M, c, scol + 1:scol + sn],
                    ALU.mult, ALU.add, [ps, ppt, prwr[c]], [prwr[c]])
            act(rawl[0:M, 0:nseg, c], ps[0:M, sn0 - 1:T:sn0], AF.Copy, [ps], [rawl])
        allprw = list(prwr)
        tt("pool", cmS[:, 0:nseg, :], carryS[:, 0:nseg, :], ppt[:, MU:MU + 15].rearrange("p (o c) -> p o c", o=1).broadcast_to([128, nseg, 15]), ALU.mult, [carryS, ppt], [cmS])
        for (si, scol, sn, sfirst, slast, sidx) in segs:
            tt("pool", prw[:, :, scol:scol + 1], prw[:, :, scol:scol + 1], cmS[:, si, :].rearrange("p (c o) -> p c o", o=1), ALU.add, allprw + [cmS], allprw)
            if kind == "s":
                stq(sshifto[sidx], rawl[:, si, :], [rawl])
            elif slast and 'c' not in SKIP:
                stq(pshift[sidx], rawl[:, 0, :], [rawl])
        if kind == "p":
            cp("pool", carryS[:, 0, :], rawl[:, 0, :], [rawl], [carryS])
        act(lin[0:64, 0, 0:T], prw[0:64, 12, 0:T], AF.Tanh, allprw, [lin])
        act(lin[64:128, 0, 0:T], prw[64:128, 12, 0:T], AF.Copy, allprw, [lin])
        act(lin[:, 1, 0:T], prw[:, 13, 0:T], AF.Sigmoid, allprw, [lin])
        act(lin[0:32, 2, 0:T], prw[0:32, 14, 0:T], AF.Sigmoid, allprw, [lin])
        rmask = rmask_p if kind == "p" else rmask_s

        def bd4(ap2d, Ct):
            return ap2d.rearrange("p (c o t) -> p c o t", c=nch, o=1).broadcast_to([128, nch, 2, Ct])

        def bdo(buf, lo, Ct):
            return buf[:, 0:nch, lo:lo + 128].rearrange("p c (h t) -> p c h t", h=2)[:, :, :, 0:Ct]

        m4 = bdm[:].rearrange("p (o h) t -> p o h t", o=1)[:, :, :, 0:C].broadcast_to([128, nch, 2, C])
        def prep(c):
            r_ = prw[:, c, 0:T]; k_ = prw[:, 4 + c, 0:T]; v_ = prw[:, 8 + c, 0:T]
            cs = slice(c * 128, (c + 1) * 128)
            ATR = ATRd[c % 2]; BTb = BTbd[c % 2]; KTb = KTbd[c % 2]; F3 = F3d[c % 2]; wcs = wcsd[c % 2]; gv = gvd[c % 2]; km = kmd[c % 2]
            wc4 = wcs[:, 0:nch].rearrange("p (c o t) -> p c o t", o=1, t=1).broadcast_to([128, nch, 2, C])
            pw = PA()
            mm(pw[:, 0:T], lwb[0:64, 0, cs], lin[0:64, 0, 0:T], [lwb, lin], [pw])
            act(sig[:, 0:T], pw[:, 0:T], AF.Sigmoid, [pw, ppt], [sig], bias=ppt[:, W0 + c:W0 + c + 1])
            pa_ = PA()
            mm(pa_[:, 0:T], lwb[64:128, 0, cs], lin[64:128, 0, 0:T], [lwb, lin], [pa_])
            act(av[:, 0:T], pa_[:, 0:T], AF.Sigmoid, [pa_, ppt], [av], bias=ppt[:, A0 + c:A0 + c + 1])
            pg = PA()
            mm(pg[:, 0:T], lwb[:, 1, cs], lin[:, 1, 0:T], [lwb, lin], [pg], True, False)
            mm(pg[:, 0:T], lwb[0:32, 2, cs], lin[0:32, 2, 0:T], [lwb, lin], [pg], False, True)
            cp("act", gv[:, 0:T], pg[:, 0:T], [pg], [gv])
            ts("pool", kk[:, 0:T], k_, ppt[:, KK + c:KK + c + 1], ALU.mult, allprw + [ppt], [kk])
            tt("pool", kkb[:, 0:T], kk[:, 0:T], kk[:, 0:T], ALU.mult, [kk], [kkb])
            pn = PA()
            mm(pn[:, 0:T], bones[:], kkb[:, 0:T], [bones, kkb], [pn])
            act(rn[:, 0:T], pn[:, 0:T], AF.Ln, [pn], [rn], bias=1e-24)
            act(rn[:, 0:T], rn[:, 0:T], AF.Exp, [rn], [rn], scale=-0.5)
            tt("dve", kk[:, 0:T], kk[:, 0:T], rn[:, 0:T], ALU.mult, [kk, rn], [kk])
            ts("dve", tmp[:, 0:T], av[:, 0:T], ppt[:, KA + c:KA + c + 1], ALU.mult, [av, ppt, omka], [tmp], omka[:, c:c + 1], ALU.add)
            tt("dve", km[:, 0:T], k_, tmp[:, 0:T], ALU.mult, allprw + [tmp], [km])
            tt("pool", bb[:, 0:T], kk[:, 0:T], av[:, 0:T], ALU.mult, [kk, av], [bb])
            ts("dve", tmp[:, 0:T], sig[:, 0:T], -0.6065306597126334, ALU.mult, [sig], [tmp])
            P.op("dve", lambda e: e.tensor_tensor_scan(out=cum[:, 0:T], data0=rmask[:, 0:T], data1=tmp[:, 0:T], initial=0.0, op0=ALU.mult, op1=ALU.add), [rmask, tmp], [cum])
            act(Wt[:, 0:T], cum[:, 0:T], AF.Exp, [cum], [Wt])
            act(Wi[:, 0:T], cum[:, 0:T], AF.Exp, [cum], [Wi], scale=-1.0)
            tt("dve", tmp[:, 0:T], cum[:, 0:T], tmp[:, 0:T], ALU.subtract, [cum, tmp], [tmp])
            act(Wp[:, 0:T], tmp[:, 0:T], AF.Exp, [tmp], [Wp])
            cp("dve", wcs[:, 0:nch], Wt[:, 0:T].rearrange("p (c t) -> p c t", t=C)[:, :, C - 1], [Wt], [wcs])
            stt(tmp[:, 0:T], kk[:, 0:T], -1.0, Wp[:, 0:T], ALU.mult, ALU.mult, [kk, Wp], [tmp])
            tt("pool", bdo(ATR, 0, C), bd4(tmp[:, 0:T], C), m4, ALU.mult, [tmp, bdm], [ATR])
            tt("dve", tmp[:, 0:T], r_, Wt[:, 0:T], ALU.mult, allprw + [Wt], [tmp])
            tt("pool", bdo(ATR, 128, C), bd4(tmp[:, 0:T], C), m4, ALU.mult, [tmp, bdm], [ATR])
            tt("dve", bb[:, 0:T], bb[:, 0:T], Wi[:, 0:T], ALU.mult, [bb, Wi], [bb])
            tt("pool", bdo(BTb, 0, C), bd4(bb[:, 0:T], C), m4, ALU.mult, [bb, bdm], [BTb])
            tt("dve", tmp[:, 0:T], km[:, 0:T], Wi[:, 0:T], ALU.mult, [km, Wi], [tmp])
            tt("pool", bdo(KTb, 0, C), bd4(tmp[:, 0:T], C), m4, ALU.mult, [tmp, bdm], [KTb])
            for j, src in ((0, None), (1, BTb), (2, KTb)):
                outv = F3[:, 0:nch, j, :].rearrange("p c (h t) -> p c h t", h=2)[:, :, :, 0:C]
                if src is None:
                    tt("pool", outv, bd4(v_, C), m4, ALU.mult, allprw + [bdm], [F3])
                else:
                    tt("pool", outv, bdo(src, 0, C), wc4, ALU.mult, [src, wcs], [F3])

        def _qkatt():
            for hp in range(5):
                ps = PA()
                for j in range(2):
                    c0 = (hp * 2 + j) * 64
                    for kc in range(8):
                        mm(ps[0:64, j * T:(j + 1) * T], winb[:, kc, c0:c0 + 64], hT[:, kc, 0:T], [winb, hT], [ps], kc == 0, kc == 7)
                qk_pair(ps, T, KG if hp == 4 else QG, hp == 4, hp)
            if kind == "p":
                col = 0
                for bi, (xap, rows, ti) in enumerate(tiles):
                    ps = PB()
                    for kc in range(8):
                        mm(ps[0:rows, 0:128], hT[:, kc, col:col + rows], winb[:, kc, 640:768], [hT, winb], [ps], kc == 0, kc == 7)
                    cp("dve", Vb[0:rows, 1 + bi, :], ps[0:rows, 0:128], [ps], [Vb])
                    if seq_last and bi == len(tiles) - 1 and 'a' not in SKIP:
                        cp("act", vf[0:rows, :], ps[0:rows, 0:128], [ps], [vf])
                        stq(pv[seqidx], vf[:, :], [vf])
                    col += rows
            stage(2)
            if kind == "p":
                for cpi in range(T // 128):
                    for kvh in range(2):
                        hasA = not (seq_first and cpi == 0)
                        qap = qT[:, kvh * 4:kvh * 4 + 4, cpi * 128:cpi * 128 + 128]
                        SA = None
                        if hasA:
                            SA = PA()
                            mm(SA[:, 0:512], kTb[:, kvh, cpi * 128:cpi * 128 + 128], qap, [kTb, qT], [SA])
                        SBb = PA()
                        mm(SBb[:, 0:512], kTb[:, kvh, 128 + cpi * 128:128 + cpi * 128 + 128], qap, [kTb, qT], [SBb])
                        pi = 0
                        attn_core(SA, SBb, qap, 128,
                                  (Vb[:, cpi, kvh * 64:kvh * 64 + 64], Vb), (Vb[:, cpi + 1, kvh * 64:kvh * 64 + 64], Vb), 128, 128,
                                  [(0, 128, 0, 64), (64, 128, 64, 128)], [(0, 64, 0, 64), (0, 128, 64, 128)],
                                  PTA[pi], PTB[pi], kvh, cpi * 128)
                cp("pool", kTb[:, :, 0:128], kTb[:, :, 128 + T - 128:128 + T], [kTb], [kTb])
                cp("pool", Vb[:, 0, :], Vb[:, T // 128, :], [Vb], [Vb])
                if seq_last and 'b' not in SKIP:
                    po = PB()
                    for kvh in range(2):
                        tr(po[:, kvh * 64:kvh * 64 + 64], kf[:, kvh, T - 128:T], identf[0:64, 0:64], [kf, identf], [po])
                    cp("act", kof[:, :], po[:, 0:128], [po], [kof])
                    stq(pk[seqidx], kof[:, :], [kof])
            else:
                for s in range(NS):
                    c0 = s * TS
                    ld("sp", ckf[:], ck[s], [ckf])
                    ld("pool", Vs[:], cv[s], [Vs])
                    po = PB()
                    for kvh in range(2):
                        tr(po[0:64, kvh * 128:kvh * 128 + 128], ckf[:, kvh * 64:kvh * 64 + 64], identf[:], [ckf, identf], [po])
                    cp("dve", kTs[:], po[0:64, 0:256].rearrange("p (k t) -> p k t", k=2), [po], [kTs])
                    ps = PB()
                    for kc in range(8):
                        mm(ps[0:TS, 0:128], hT[:, kc, c0:c0 + TS], winb[:, kc, 640:768], [hT, winb], [ps], kc == 0, kc == 7)
                    cp("dve", Vn[0:TS, :], ps[0:TS, 0:128], [ps], [Vn])
                    cp("act", vf[0:TS, :], ps[0:TS, 0:128], [ps], [vf])
                    stq(sv[s], vf[0:TS, :], [vf])
                    po2 = PB()
                    for kvh in range(2):
                        tr(po2[0:TS, kvh * 64:kvh * 64 + 64], kf[:, kvh, c0:c0 + TS], identf[0:64, 0:64], [kf, identf], [po2])
                    cp("act", kof[0:TS, :], po2[0:TS, 0:128], [po2], [kof])
                    stq(sk[s], kof[0:TS, :], [kof])
                    for kvh in range(2):
                        qap = qT[:, kvh * 4:kvh * 4 + 4, c0:c0 + TS]
                        SA = PA()
                        mm(SA[:, 0:4 * TS], kTs[:, kvh, :], qap, [kTs, qT], [SA])
                        SBb = PA()
                        mm(SBb[0:TS, 0:4 * TS], kTb[:, kvh, 128 + c0:128 + c0 + TS], qap, [kTb, qT], [SBb])
                        attn_core(SA, SBb, qap, TS, (Vs[:, kvh * 64:kvh * 64 + 64], Vs), (Vn[0:TS, kvh * 64:kvh * 64 + 64], Vn), 128, TS,
                                  [(0, 128, 0, TS)], [(0, TS, 0, TS)], PSA, PSB, kvh, c0)
            stage(3)

        rr["cur"] = "side"
        side0 = P.record(lambda: prep(0))
        rr["cur"] = "main"
        main0 = P.record(_qkatt)
        rr["cur"] = "all"
        P.replay_merged(main0, side0)
        def chain(c):
            r_ = prw[:, c, 0:T]; k_ = prw[:, 4 + c, 0:T]; v_ = prw[:, 8 + c, 0:T]
            cs = slice(c * 128, (c + 1) * 128)
            ATR = ATRd[c % 2]; BTb = BTbd[c % 2]; KTb = KTbd[c % 2]; F3 = F3d[c % 2]; wcs = wcsd[c % 2]; gv = gvd[c % 2]; km = kmd[c % 2]
            wc4 = wcs[:, 0:nch].rearrange("p (c o t) -> p c o t", o=1, t=1).broadcast_to([128, nch, 2, C])
            m2 = masks[:, 0:2, :].rearrange("p a t -> p (a t)")
            pts = []
            for ch in range(nch):
                pt = PB(); ptv = pt.t.bitcast(BF16)
                for j in range(3):
                    tr(ptv[:, j * 128:(j + 1) * 128], F3[:, ch, j, :], identb[:], [F3, identb], [pt])
                pts.append((pt, ptv))
            for ch in range(nch):
                pt, ptv = pts[ch]
                cp("act", T3[:, ch, :, :], ptv[:, 0:384].rearrange("p (j t) -> p j t", j=3), [pt], [T3.c(ch)])
            p1s = []
            for ch in range(nch):
                p1 = PB()
                mm(p1[:, 0:256], BTb[:, ch, :], ATR[:, ch, :], [BTb, ATR], [p1])
                p1s.append(p1)
            for ch in range(nch):
                tt("dve", LTR[:, ch, :], p1s[ch][:, 0:256], m2, ALU.mult, [p1s[ch], masks], [LTR.c(ch)])
            p2s = []
            for ch in range(nch):
                p2 = PB()
                mm(p2[:, 0:256], KTb[:, ch, :], ATR[:, ch, :], [KTb, ATR], [p2])
                p2s.append(p2)
            for ch in range(nch):
                tt("dve", AKR[:, ch, :], p2s[ch][:, 0:256], m2, ALU.mult, [p2s[ch], masks], [AKR.c(ch)])
            p3s = []
            for ch in range(nch):
                p3 = PB()
                mm(p3[:, 0:128], ATR[:, ch, 0:128], BTb[:, ch, :], [ATR, BTb], [p3])
                p3s.append(p3)
            for ch in range(nch):
                tt("dve", XX[0][:, ch, 0:128], p3s[ch][:, 0:128], masks[:, 2, :], ALU.mult, [p3s[ch], masks], [XX[0].c(ch)])
                cp("pool", XX[0][:, ch, 128:256], LTR[:, ch, 0:128], [LTR.c(ch)], [XX[0].c(ch)])
                tt("pool", MT[0][:, ch, :], LTR[:, ch, 0:128], identb[:], ALU.add, [LTR.c(ch), identb], [MT[0].c(ch)])
            nlev = 5 if C == 64 else 3
            for lev in range(1, nlev + 1):
                a_, b_ = XX[(lev - 1) % 2], XX[lev % 2]
                ma, mb = MT[(lev - 1) % 2], MT[lev % 2]
                pxs = []
                for ch in range(nch):
                    px = PB()
                    mm(px[:, 0:128], a_[:, ch, 128:256], a_[:, ch, 0:128], [a_.c(ch)], [px])
                    if lev < nlev:
                        mm(px[:, 128:256], a_[:, ch, 0:128], a_[:, ch, 128:256], [a_.c(ch)], [px])
                    pxs.append(px)
                for ch in range(nch):
                    ee = "act" if ch % 2 == 0 else "dve"
                    if lev < nlev:
                        cp(ee, b_[:, ch, :], pxs[ch][:, 0:256], [pxs[ch]], [b_.c(ch)])
                    else:
                        cp(ee, b_[:, ch, 0:128], pxs[ch][:, 0:128], [pxs[ch]], [b_.c(ch)])
                pms = []
                for ch in range(nch):
                    pm = PB()
                    mm(pm[:, 0:128], b_[:, ch, 0:128], ma[:, ch, :], [b_.c(ch), ma.c(ch)], [pm])
                    pms.append(pm)
                for ch in range(nch):
                    tt("dve", mb[:, ch, :], pms[ch][:, 0:128], ma[:, ch, :], ALU.add, [pms[ch], ma.c(ch)], [mb.c(ch)])
            MTf = MT[nlev % 2]
            for ch in range(nch):
                if kind == "s":
                    sidx = ch
                    ld("sp", Sld[:], swkv[sidx, c], [Sld])
                    tt("pool", Sbd[:].rearrange("p (h t) -> p h t", h=2), Sld[:].rearrange("p (o t) -> p o t", o=1).broadcast_to([128, 2, 64]), bdm[:], ALU.mult, [Sld, bdm], [Sbd])
                    pS = PB()
                    tr(pS[:, 0:128], Sbd[:], identf[:], [Sbd, identf], [pS])
                    cp("dve", Tf[c][:], pS[:, 0:128], [pS], [Tf[c]])
                    cp("act", Tb[c][tpar[c]][:], pS[:, 0:128], [pS], [Tb[c][tpar[c]]])
                elif seq_first and ch == 0:
                    memset("pool", Tf[c][:], 0.0, [Tf[c]])
                    memset("pool", Tb[c][tpar[c]][:], 0.0, [Tb[c][tpar[c]]])
                Told = Tb[c][tpar[c]]; Tnew = Tb[c][1 - tpar[c]]
                pG = PB()
                mm(pG[:, 0:128], ATR[:, ch, 0:128], Told[:], [ATR, Told], [pG], True, False)
                mm(pG[:, 0:128], AKR[:, ch, 0:128], T3[:, ch, 0, :], [AKR.c(ch), T3.c(ch)], [pG], False, True)
                cp("dve", Zb[:], pG[:, 0:128], [pG], [Zb])
                pU = PB()
                mm(pU[:, 0:128], MTf[:, ch, :], Zb[:], [MTf.c(ch), Zb], [pU])
                cp("act", Ub[:], pU[:, 0:128], [pU], [Ub])
                pT = PB()
                mm(pT[:, 0:128], T3[:, ch, 1, :], Ub[:], [T3.c(ch), Ub], [pT], True, False)
                mm(pT[:, 0:128], T3[:, ch, 2, :], T3[:, ch, 0, :], [T3.c(ch)], [pT], False, True)
                pY = PB()
                mm(pY[:, 0:128], Told[:], ATR[:, ch, 128:256], [Told, ATR], [pY], True, False)
                mm(pY[:, 0:128], Ub[:], LTR[:, ch, 128:256], [Ub, LTR.c(ch)], [pY], False, False)
                mm(pY[:, 0:128], T3[:, ch, 0, :], AKR[:, ch, 128:256], [T3.c(ch), AKR.c(ch)], [pY], False, True)
                stt(Tnew[:], Tf[c][:], wcs[:, ch:ch + 1], pT[:, 0:128], ALU.mult, ALU.add, [Tf[c], wcs, pT], [Tnew])
                stt(Tf[c][:], Tf[c][:], wcs[:, ch:ch + 1], pT[:, 0:128], ALU.mult, ALU.add, [Tf[c], wcs, pT], [Tf[c]])
                tpar[c] = 1 - tpar[c]
                cp("dve", yTp[0:64, ch * C:ch * C + C], pY[0:64, 0:C], [pY], [yTp])
                cp("act", yTp[64:128, ch * C:ch * C + C], pY[64:128, 64:64 + C], [pY], [yTp])
                if (kind == "s" or (seq_last and ch == nch - 1)) and 'd' not in SKIP:
                    pS = PB()
                    tr(pS[:, 0:128], Tf[c][:], identf[:], [Tf[c], identf], [pS])
                    cp("dve", Sout[:], pS[:, 0:128], [pS], [Sout])
                    dst = swkvo if kind == "s" else pwkv
                    di = ch if kind == "s" else seqidx
                    stq(dst[di, 2 * c], Sout[0:64, 0:64], [Sout])
                    stq(dst[di, 2 * c + 1], Sout[64:128, 64:128], [Sout])

        def post(c):
            r_ = prw[:, c, 0:T]; k_ = prw[:, 4 + c, 0:T]; v_ = prw[:, 8 + c, 0:T]
            cs = slice(c * 128, (c + 1) * 128)
            ATR = ATRd[c % 2]; BTb = BTbd[c % 2]; KTb = KTbd[c % 2]; F3 = F3d[c % 2]; wcs = wcsd[c % 2]; gv = gvd[c % 2]; km = kmd[c % 2]
            wc4 = wcs[:, 0:nch].rearrange("p (c o t) -> p c o t", o=1, t=1).broadcast_to([128, nch, 2, C])
            cp("act", ybfp[:, 0:T], yTp[:, 0:T], [yTp], [ybfp])
            act(y2bf[:, 0:T], yTp[:, 0:T], AF.Square, [yTp], [y2bf])
            pmn = PA(); pe2 = PA()
            mm(pmn[:, 0:T], bones64[:], ybfp[:, 0:T], [bones64, ybfp], [pmn])
            mm(pe2[:, 0:T], bones64[:], y2bf[:, 0:T], [bones64, y2bf], [pe2])
            act(msqp[:, 0:T], pmn[:, 0:T], AF.Square, [pmn], [msqp])
            tt("dve", varp[:, 0:T], pe2[:, 0:T], msqp[:, 0:T], ALU.subtract, [pe2, msqp], [varp])
            act(varp[:, 0:T], varp[:, 0:T], AF.Ln, [varp], [varp], bias=64e-5)
            act(varp[:, 0:T], varp[:, 0:T], AF.Exp, [varp], [varp], scale=-0.5)
            tt("dve", ycp[:, 0:T], yTp[:, 0:T], pmn[:, 0:T], ALU.subtract, [yTp, pmn], [ycp])
            tt("dve", ycp[:, 0:T], ycp[:, 0:T], varp[:, 0:T], ALU.mult, [ycp, varp], [ycp])
            ts("dve", ycp[:, 0:T], ycp[:, 0:T], ppt[:, LG + c:LG + c + 1], ALU.mult, [ycp, ppt], [ycp], ppt[:, LB + c:LB + c + 1], ALU.add)
            stt(rkr[:, 0:T], r_, ppt[:, RK + c:RK + c + 1], km[:, 0:T], ALU.mult, ALU.mult, allprw + [ppt, km], [rkr])
            pbs = PA()
            mm(pbs[:, 0:T], bones[:], rkr[:, 0:T], [bones, rkr], [pbs])
            tt("dve", tmp2[:, 0:T], pbs[:, 0:T], v_, ALU.mult, [pbs] + allprw, [tmp2])
            tt("dve", ycp[:, 0:T], ycp[:, 0:T], tmp2[:, 0:T], ALU.add, [ycp, tmp2], [ycp])
            tt("dve", mixT[:, c, 0:T], ycp[:, 0:T], gv[:, 0:T], ALU.mult, [ycp, gv], [mixT])

        for c in range(4):
            def _main(c=c):
                chain(c)
                post(c)
            if c < 3:
                rr["cur"] = "side"
                side = P.record(lambda c=c: prep(c + 1))
                rr["cur"] = "main"
                main = P.record(_main)
                rr["cur"] = "all"
                P.replay_merged(main, side)
            else:
                _main()
        stage(4)
        if before_E is not None:
            P.mark()
            before_E()
        col = 0
        for (xap, rows, ti) in tiles:
            for half in range(2):
                po = PA()
                hs = slice(half * 512, half * 512 + 512)
                for kc in range(4):
                    mm(po[0:rows, :], attT[:, kc, col:col + rows], woutb[:, kc, hs], [attT, woutb], [po], kc == 0, False)
                for kc in range(4):
                    mm(po[0:rows, :], mixT[:, kc, col:col + rows], woutb[:, 4 + kc, hs], [mixT, woutb], [po], False, kc == 3)
                tt("dve", x1[0:rows, ti, hs], x1[0:rows, ti, hs], po[0:rows, :], ALU.add, [x1r[ti], po], [x1r[ti]])
            rmsnorm_T(x1[0:rows, ti, :], x1r[ti], rows, g2bc, h2T, h2col0 + col)
            col += rows

    def mlp(T, tiles, outs, hcol0):
        for sp2 in range(NSL // 2):
            for j in range(2):
                s = 2 * sp2 + j
                wu = wupr[j]; wd = wdnr[j]
                ld("sp", wu[:], wups[s], [wu], [wsu[s]]); ld("sp", wd[:], wdns[s], [wd], [wsd[s]])
                u = uT[j]
                for m in range(2):
                    pu = PA()
                    for kc in range(8):
                        mm(pu[:, 0:T], wu[:, kc, m * 128:(m + 1) * 128], h2T[:, kc, hcol0:hcol0 + T], [wu, h2T], [pu], kc == 0, kc == 7)
                    relu_m = relu_ms[m]
                    act(relu_m[:, 0:T], pu[:, 0:T], AF.Relu, [pu], [relu_m])
                    tt("pool", u[:, m, 0:T], relu_m[:, 0:T], relu_m[:, 0:T], ALU.mult, [relu_m], [u])
            for (rows, ti, c0) in tiles:
                for half in range(2):
                    pd = PA()
                    hs = slice(half * 512, half * 512 + 512)
                    for j in range(2):
                        for m in range(2):
                            mm(pd[0:rows, :], uT[j][:, m, c0:c0 + rows], wdnr[j][:, m, hs], [uT[j], wdnr[j]], [pd], j == 0 and m == 0, j == 1 and m == 1)
                    tt("dve", x1[0:rows, ti, hs], x1[0:rows, ti, hs], pd[0:rows, :], ALU.add, [x1r[ti], pd], [x1r[ti]])
        for (rows, ti, c0), oap in zip(tiles, outs):
            stq(oap, x1[0:rows, ti, :], [x1r[ti]])

    jobs = []
    for sq_ in range(NSEQ):
        ng = SEQ // GT
        for g in range(ng):
            r0 = sq_ * SEQ + g * GT
            par = len(jobs) % 2
            tiles = [(xp[r0 + i * 128:r0 + (i + 1) * 128, :], 128, par * 2 + i) for i in range(GT // 128)]
            segs = [(0, 0, GT, g == 0, g == ng - 1, sq_)]
            mix = (lambda do_A, before_E, tiles=tiles, segs=segs, g=g, par=par, ng=ng, sq_=sq_:
                   mixer_group("p", GT, tiles, segs, g * GT, par * GT, g == 0, g == ng - 1, sq_, do_A, before_E))
            ffn = (lambda r0=r0, par=par:
                   mlp(GT, [(128, par * 2 + i, i * 128) for i in range(2)], [yp[r0 + i * 128:r0 + (i + 1) * 128, :] for i in range(2)], par * GT))
            jobs.append((mix, ffn, (lambda tiles=tiles: stage_A(tiles))))
    if NS > 0:
        Ts = NS * TS
        pars = len(jobs) % 2
        tiles_s = [(xs[0:Ts, :], Ts, pars * 2)]

        def mix_s(do_A, before_E):
            for b_ in ATRd + BTbd + KTbd + F3d:
                memset("pool", b_[:], 0.0, [b_])
            mixer_group("s", Ts, tiles_s, [(s, s * TS, TS, True, True, s) for s in range(NS)], SEQ, pars * GT, True, True, 0, do_A, before_E)
        jobs.append((mix_s, lambda: mlp(Ts, [(Ts, pars * 2, 0)], [ys[0:Ts, :]], pars * GT), lambda: stage_A(tiles_s)))
    prev_ffn = None
    for i, (mix, ffn, afn) in enumerate(jobs):
        nxtA = jobs[i + 1][2] if i + 1 < len(jobs) else None
        if prev_ffn is None:
            rr["cur"] = "all"
            main = P.record(lambda: mix(True, nxtA))
            P.replay_merged(main, ffn_convert)
        else:
            rr["cur"] = "mlp"
            side = P.record(prev_ffn)
            rr["cur"] = "all"
            main = P.record(lambda: mix(False, nxtA))
            P.replay_merged(main, side)
        prev_ffn = ffn
    rr["cur"] = "mlp"
    prev_ffn()
    P.finish()
    P.emit(st)
    st.close()
    return nc


def _feat(v, n):
    return np.ascontiguousarray(v.reshape(n, 128).T)


def _feat15(v):
    out = np.zeros((128, 15), np.float32)
    out[:, :14] = v[:1792].reshape(14, 128).T
    out[:32, 14] = v[1792:1824]
    return out


def _unfeat15(a):
    return np.concatenate([a[:, :14].T.reshape(-1), a[:32, 14]])


def _consts(SEQ, TS, NS):
    ident = np.eye(128, dtype=np.float32)
    rperm = np.zeros((64, 64), np.float32)
    for m in range(32):
        rperm[m + 32, m] = -1.0
        rperm[m, m + 32] = 1.0
    idx = np.arange(128)
    same = (idx[:, None] // 64) == (idx[None, :] // 64)
    r, c = idx[:, None] % 64, idx[None, :] % 64
    masks = np.stack([same & (r < c), same & (r <= c), same & (r > c)], 1).astype(np.float32)
    half = 32
    inv = (10000.0 ** (-np.arange(half, dtype=np.float32) / half)).astype(np.float32)
    pos = np.concatenate([np.arange(SEQ)] + [PAST + np.arange(TS)] * NS).astype(np.float32)
    ang = pos[None, :] * inv[:, None]
    cos = np.cos(ang).astype(np.float32); sin = np.sin(ang).astype(np.float32)
    return ident, rperm, masks, np.concatenate([cos, cos], 0), np.concatenate([sin, -sin], 0)


_CACHE = {}
_HOOK = None


def kernel(x_prompt, x_sample, cache_attn_k, cache_attn_v, state_rwkv_wkv, state_rwkv_shift,
           ln1_g, w_in, q_norm_g, k_norm_g, attn_sinks, shift_mu, decay_w0, decay_w2, iclr_a0, iclr_a2,
           gate_g2, k_k, k_a, r_k, lnx_g, lnx_b, w_out, ln2_g, w_up, w_down):
    f = lambda a: np.ascontiguousarray(np.asarray(a, dtype=np.float32))
    x_prompt, x_sample = f(x_prompt), f(x_sample)
    B, SEQ, _ = x_prompt.shape
    BS, TS, _ = x_sample.shape
    NSEQ, NS = B // NCORES, BS // NCORES
    key = (NSEQ, SEQ, NS, TS)
    if key not in _CACHE:
        _CACHE[key] = build(*key)
    nc = _CACHE[key]
    ident, rperm, masks, ropec, ropes = _consts(SEQ, TS, NS)
    pp = np.zeros((128, 48), np.float32)
    pp[:, 0:15] = _feat15(f(shift_mu)[0])
    for off, v in ((15, decay_w0), (19, iclr_a0), (23, k_k), (27, k_a), (31, f(r_k).reshape(1, 512)), (35, lnx_g), (39, lnx_b)):
        pp[:, off:off + 4] = _feat(f(v)[0], 4)
    pp[:64, 43] = f(q_norm_g)[0]
    pp[:64, 44] = f(k_norm_g)[0]
    common = dict(g1=_feat(f(ln1_g)[0], 8), g2=_feat(f(ln2_g)[0], 8), w_in=f(w_in)[0], w_out=f(w_out)[0], w_up=f(w_up)[0], w_down=f(w_down)[0],
                  dw2=f(decay_w2)[0], ia2=f(iclr_a2)[0], gg2=f(gate_g2)[0], pp=pp, sinks=f(attn_sinks),
                  ident=ident, rperm=rperm, masks=masks, ropec=ropec, ropes=ropes)
    ckf = f(cache_attn_k)[0].reshape(BS, 128, 128); cvf = f(cache_attn_v)[0].reshape(BS, 128, 128)
    wkv = f(state_rwkv_wkv)[0].reshape(BS, 4, 128, 64)
    shf = np.stack([_feat15(r) for r in f(state_rwkv_shift)[0]], 0)
    in_maps = []
    for c in range(NCORES):
        m = dict(common)
        m["xp"] = x_prompt[c * NSEQ:(c + 1) * NSEQ].reshape(NSEQ * SEQ, D)
        m["xs"] = x_sample[c * NS:(c + 1) * NS].reshape(NS * TS, D)
        m["ck"] = ckf[c * NS:(c + 1) * NS]; m["cv"] = cvf[c * NS:(c + 1) * NS]
        m["swkv"] = wkv[c * NS:(c + 1) * NS]; m["sshift"] = shf[c * NS:(c + 1) * NS]
        in_maps.append(m)
    if _HOOK is not None:
        R = _HOOK(nc, in_maps)
    else:
        R = run_bass_kernel_spmd(nc, in_maps, core_ids=list(range(NCORES))).results
    cat = lambda k: np.concatenate([np.asarray(r[k], dtype=np.float32) for r in R], 0)
    y_p = cat("yp").reshape(B, SEQ, D); y_s = cat("ys").reshape(BS, TS, D)
    p_k = cat("pk").reshape(1, B, 128, 2, 64); p_v = cat("pv").reshape(1, B, 128, 2, 64)
    p_wkv = cat("pwkv").reshape(1, B, 8, 64, 64)
    p_sh = np.stack([_unfeat15(a) for a in cat("pshift")], 0).reshape(1, B, RWC)
    s_k = cat("sk").reshape(1, BS, TS, 2, 64); s_v = cat("sv").reshape(1, BS, TS, 2, 64)
    s_wkv = cat("swkvo").reshape(1, BS, 8, 64, 64)
    s_sh = np.stack([_unfeat15(a) for a in cat("sshifto")], 0).reshape(1, BS, RWC)
    return (y_p, y_s, p_k, p_v, p_wkv, p_sh, s_k, s_v, s_wkv, s_sh)
```

```python
import os
import numpy as np
from contextlib import ExitStack
import concourse.bass as bass
import concourse.mybir as mybir
from concourse.bass_utils import run_bass_kernel_spmd

F32 = mybir.dt.float32
BF16 = mybir.dt.bfloat16
ALU = mybir.AluOpType
AF = mybir.ActivationFunctionType

D = 1024
INC = 2592
DFF = 4096
RWC = 1824
PAST = 4096
NCORES = 8


class Res:
    __slots__ = ("name", "lw", "rd", "psum", "last_kind", "last_alloc")

    def __init__(self, name):
        self.name = name
        self.lw = None
        self.rd = {}
        self.psum = False
        self.last_kind = None
        self.last_alloc = None


class Buf:
    def __init__(self, t, name):
        self.t = t
        self.r = Res(name)
        self.cr = [Res(name + "_c%d" % i) for i in range(4)]

    def c(self, ch):
        return self.cr[ch]

    def __getitem__(self, i):
        return self.t[i]


def _res(xs):
    return [getattr(x, "r", x) for x in xs]


class Prog:
    COMPUTE = ("pe", "act", "dve", "pool")
    QUEUES = ("sp", "act", "pool")

    def __init__(self, nc, ndma_sems=16):
        self.nc = nc
        self.ops = {e: [] for e in ("pe", "act", "dve", "pool", "sp")}
        self.count = {e: 0 for e in self.COMPUTE}
        self.seen = {e: {} for e in ("pe", "act", "dve", "pool", "sp")}
        self.ndma_sems = ndma_sems
        self.dma_n = {}
        self.dma_rr = {q: 0 for q in self.QUEUES}
        self.dma_final = []
        self.capture = None

    def record(self, f):
        outer = self.capture
        self.capture = []
        try:
            f()
        finally:
            lst, self.capture = self.capture, outer
        return lst

    def replay_merged(self, main, side):
        cut = len(main)
        for i, it in enumerate(main):
            if it[0] == "mark":
                cut = i
                break
        head = main[:cut]
        n1, n2 = len(head), len(side)
        j = 0
        for i, it in enumerate(head):
            self._replay(it)
            want = ((i + 1) * n2) // max(n1, 1)
            while j < want:
                self._replay(side[j]); j += 1
        while j < n2:
            self._replay(side[j]); j += 1
        for it in main[cut + 1:]:
            if it[0] != "mark":
                self._replay(it)

    def mark(self):
        if self.capture is not None:
            self.capture.append(("mark",))

    def _replay(self, it):
        if it[0] == "mark":
            if self.capture is not None:
                self.capture.append(it)
            return
        if it[0] == "op":
            self.op(it[1], it[2], it[3], it[4])
        else:
            self.dma(it[1], it[2], it[3], it[4], it[5])

    def _deps(self, reads, writes):
        deps = set()
        for r in reads:
            if r.lw is not None:
                deps.add(r.lw)
        for w in writes:
            if w.lw is not None:
                deps.add(w.lw)
            for s, i in w.rd.items():
                deps.add((s, i))
        return deps

    def _emit_waits(self, eng, deps):
        seen = self.seen[eng]
        best = {}
        for s, i in deps:
            if seen.get(s, 0) >= i:
                continue
            if best.get(s, 0) < i:
                best[s] = i
        for s, i in best.items():
            seen[s] = i
            self.ops[eng].append(("wait", s, i))

    def _mark(self, me, reads, writes):
        for r in reads:
            if r.rd.get(me[0], 0) < me[1]:
                r.rd[me[0]] = me[1]
        for w in writes:
            w.lw = me
            w.rd = {}

    def op(self, eng, fn, reads=(), writes=()):
        if self.capture is not None:
            self.capture.append(("op", eng, fn, list(reads), list(writes), False))
            return None
        for x in writes:
            a_ = getattr(x, "alloc", None)
            if a_ is not None:
                if x.r.last_kind == "w" and x.r.last_alloc != a_:
                    raise RuntimeError("PSUM bank %s overwritten before its previous contents were consumed" % x.r.name)
                x.r.last_kind = "w"; x.r.last_alloc = a_
        for x in reads:
            if getattr(x, "alloc", None) is not None:
                x.r.last_kind = "r"
        reads = _res(reads)
        writes = _res(writes)
        writes = writes + [r for r in reads if r.psum and r not in writes]
        deps = self._deps(reads, writes)
        if eng == "pe":
            deps = {d for d in deps if d[0] != "pe"}
        self._emit_waits(eng, deps)
        self.count[eng] += 1
        me = (eng, self.count[eng])
        self.ops[eng].append(("op", fn, me))
        self._mark(me, reads, writes)
        return me

    def dma(self, q, fn, reads=(), writes=(), final=False):
        if self.capture is not None:
            self.capture.append(("dma", q, fn, list(reads), list(writes), final))
            return None
        reads = _res(reads)
        writes = _res(writes)
        slot = (q, self.dma_rr[q] % self.ndma_sems)
        self.dma_rr[q] += 1
        n = self.dma_n.get(slot, 0)
        deps = self._deps(reads, writes)
        if n > 0:
            deps.add((("d",) + slot, n))
        self._emit_waits(q, deps)
        self.dma_n[slot] = n + 1
        me = (("d",) + slot, n + 1)
        self.ops[q].append(("dma", fn, me))
        self._mark(me, reads, writes)
        if final:
            self.dma_final.append(me)
        return me

    def finish(self):
        self._emit_waits("sp", set(self.dma_final))

    def emit(self, stack):
        nc = self.nc
        sems = {}
        for e in self.COMPUTE:
            sems[e] = stack.enter_context(nc.semaphore("s_" + e))
        for slot in self.dma_n:
            sems[("d",) + slot] = stack.enter_context(nc.semaphore("d_%s_%d" % slot))
        block = stack.enter_context(nc.Block())
        engmap = {"pe": "tensor", "act": "scalar", "dve": "vector", "pool": "gpsimd", "sp": "sync"}

        def run(e, eng):
            for it in self.ops[e]:
                if it[0] == "wait":
                    s, i = it[1], it[2]
                    eng.wait_ge(sems[s], i * 16 if isinstance(s, tuple) else i)
                elif it[0] == "op":
                    it[1](eng).then_inc(sems[it[2][0]], 1)
                else:
                    it[1](eng).then_inc(sems[it[2][0]], 16)

        for e, attr in engmap.items():
            if self.ops[e]:
                getattr(block, attr)(lambda eng, e=e: run(e, eng))


def build(NSEQ, SEQ, NS, TS):
    nc = bass.Bass("TRN2", target_bir_lowering=False)
    GT = 256
    ST = 512
    NPOS = SEQ + NS * TS

    def din(name, shape, dt=F32):
        return nc.dram_tensor(name, list(shape), dt, kind="ExternalInput").ap()

    def dout(name, shape):
        return nc.dram_tensor(name, list(shape), F32, kind="ExternalOutput").ap()

    xp = din("xp", [NSEQ * SEQ, D]); xs = din("xs", [NS * TS, D])
    ck = din("ck", [NS, 128, 128]); cv = din("cv", [NS, 128, 128])
    swkv = din("swkv", [NS, 4, 128, 64]); sshift = din("sshift", [NS, 128, 15])
    g1 = din("g1", [128, 8]); g2 = din("g2", [128, 8])
    w_in = din("w_in", [D, INC]); w_out = din("w_out", [D, D])
    w_up = din("w_up", [D, DFF]); w_down = din("w_down", [DFF, D])
    dw2 = din("dw2", [64, 512]); ia2 = din("ia2", [64, 512]); gg2 = din("gg2", [160, 512])
    pp = din("pp", [128, 48]); sinks = din("sinks", [1, 8])
    identd = din("ident", [128, 128]); rpermd = din("rperm", [64, 64]); masksd = din("masks", [128, 3, 128])
    ropecd = din("ropec", [64, NPOS]); ropesd = din("ropes", [64, NPOS])

    yp = dout("yp", [NSEQ * SEQ, D]); ys = dout("ys", [NS * TS, D])
    pk = dout("pk", [NSEQ, 128, 128]); pv = dout("pv", [NSEQ, 128, 128])
    pwkv = dout("pwkv", [NSEQ, 8, 64, 64]); pshift = dout("pshift", [NSEQ, 128, 15])
    sk = dout("sk", [NS, TS, 128]); sv = dout("sv", [NS, TS, 128])
    swkvo = dout("swkvo", [NS, 8, 64, 64]); sshifto = dout("sshifto", [NS, 128, 15])

    NSL = 16
    wups = nc.dram_tensor("wups", [NSL, 128, 8, 256], BF16, kind="Internal").ap()
    wdns = nc.dram_tensor("wdns", [NSL, 128, 2, D], BF16, kind="Internal").ap()

    st = ExitStack()
    P = Prog(nc)
    STOP = int(os.environ.get('KSTOP', '99'))
    SKIP = os.environ.get('KSKIP', '')

    class StopBuild(Exception):
        pass

    def stage(n):
        if STOP <= n:
            raise StopBuild()

    allbufs = []

    def sb(name, shape, dt=F32):
        b = Buf(st.enter_context(nc.sbuf_tensor("s_" + name, list(shape), dt)), name)
        allbufs.append(b)
        return b

    banks = [Buf(st.enter_context(nc.psum_tensor("pa%d" % i, [128, 512], F32)), "pa%d" % i) for i in range(8)]
    for b_ in banks:
        b_.r.psum = True
    rings = {"all": list(range(6)), "main": list(range(5)), "side": [5], "mlp": [6, 7]}
    rr = {"all": 0, "main": 0, "side": 0, "mlp": 0, "cur": "all"}

    class _View:
        def __init__(self, bank, ncol, alloc):
            self.t = bank.t[:, 0:ncol]
            self.r = bank.r
            self.alloc = alloc

        def __getitem__(self, i):
            return self.t[i]

    def PA(ncol=512):
        k = rr["cur"]
        ring = rings[k]
        b = banks[ring[rr[k] % len(ring)]]; rr[k] += 1
        rr["alloc"] = rr.get("alloc", 0) + 1
        return _View(b, ncol, rr["alloc"])

    def PB():
        return PA(256)

    def tt(eng, out, a, b, op, R, W):
        P.op(eng, lambda e: e.tensor_tensor(out=out, in0=a, in1=b, op=op), R, W)

    def ts(eng, out, a, s1, op0, R, W, s2=None, op1=None):
        if op1 is None:
            P.op(eng, lambda e: e.tensor_scalar(out=out, in0=a, scalar1=s1, scalar2=None, op0=op0), R, W)
        else:
            P.op(eng, lambda e: e.tensor_scalar(out=out, in0=a, scalar1=s1, scalar2=s2, op0=op0, op1=op1), R, W)

    def stt(out, a, s, b, op0, op1, R, W):
        P.op("dve", lambda e: e.scalar_tensor_tensor(out=out, in0=a, scalar=s, in1=b, op0=op0, op1=op1), R, W)

    def act(out, in_, func, R, W, **kw):
        P.op("act", lambda e: e.activation(out=out, in_=in_, func=func, **kw), R, W)

    def cp(eng, out, in_, R, W):
        if eng == "act":
            act(out, in_, AF.Copy, R, W)
        else:
            P.op(eng, lambda e: e.tensor_copy(out=out, in_=in_), R, W)

    def mm(out, lhsT, rhs, R, W, start=True, stop=True):
        P.op("pe", lambda e: e.matmul(out, lhsT=lhsT, rhs=rhs, start=start, stop=stop), R, W)

    def tr(out, in_, ident_ap, R, W):
        P.op("pe", lambda e: e.transpose(out=out, in_=in_, identity=ident_ap), R, W)

    def memset(eng, ap, val, W):
        P.op(eng, lambda e: e.memset(ap, val), (), W)

    def ld(q, out, in_, W, R=()):
        P.dma(q, lambda e: e.dma_start(out=out, in_=in_), R, W)

    def stq(out, in_, R):
        P.dma("sp", lambda e: e.dma_start(out=out, in_=in_), R, (), final=True)

    identf = sb("identf", [128, 128]); identb = sb("identb", [128, 128], BF16)
    rperm = sb("rperm", [64, 64]); masks = sb("masks", [128, 3, 128])
    onesb = sb("onesb", [128, 128], BF16); bones = sb("bones", [128, 128], BF16); bones64 = sb("bones64", [128, 128], BF16)
    bdm = sb("bdm", [128, 2, 64])
    g1bc = sb("g1bc", [128, 8]); g2bc = sb("g2bc", [128, 8])
    ppt = sb("ppt", [128, 48]); omka = sb("omka", [128, 4]); omu = sb("omu", [128, 15]); esink = sb("esink", [64, 8])
    lwb = sb("lwb", [128, 3, 512], BF16)
    winb = sb("winb", [128, 8, INC], BF16); woutb = sb("woutb", [128, 8, D], BF16)
    rmask_p = sb("rmask_p", [128, GT]); rmask_s = sb("rmask_s", [128, 64])
    wsu = [Res("wsu%d" % i) for i in range(16)]; wsd = [Res("wsd%d" % i) for i in range(16)]

    ld("sp", identf[:], identd[:, :], [identf]); ld("sp", rperm[:], rpermd[:, :], [rperm])
    ld("sp", masks[:], masksd[:, :, :], [masks]); ld("sp", ppt[:], pp[:, :], [ppt])
    ld("sp", g1bc[:], g1[:, :], [g1bc]); ld("sp", g2bc[:], g2[:, :], [g2bc])
    ld("sp", esink[:], sinks.broadcast_to([64, 8]), [esink])
    ld("pool", lwb[0:64, 0, :], dw2[:, :], [lwb]); ld("pool", lwb[64:128, 0, :], ia2[:, :], [lwb])
    ld("pool", lwb[:, 1, :], gg2[0:128, :], [lwb]); ld("pool", lwb[0:32, 2, :], gg2[128:160, :], [lwb])
    for hlf in range(2):
        ld("pool", winb[:, :, hlf * 1296:(hlf + 1) * 1296],
           w_in[:, hlf * 1296:(hlf + 1) * 1296].rearrange("(c p) n -> p c n", p=128), [winb])
    ld("pool", woutb[:], w_out.rearrange("(c p) n -> p c n", p=128), [woutb])
    def _convert_ffn_weights():
        for s in range(NSL):
            ld("pool", wups[s], w_up[:, s * 256:(s + 1) * 256].rearrange("(c p) n -> p c n", p=128), [wsu[s]])
            ld("pool", wdns[s], w_down[s * 256:(s + 1) * 256, :].rearrange("(c p) n -> p c n", p=128), [wsd[s]])
    ffn_convert = P.record(_convert_ffn_weights)
    cp("dve", identb[:], identf[:], [identf], [identb])
    memset("dve", onesb[:], 1.0, [onesb])
    memset("pool", bdm[:], 0.0, [bdm]); memset("pool", bdm[0:64, 0, :], 1.0, [bdm]); memset("pool", bdm[64:128, 1, :], 1.0, [bdm])
    cp("dve", bones[:], bdm[:].rearrange("p h t -> p (h t)"), [bdm], [bones])
    ts("dve", bones64[:], bones[:], 1.0 / 64, ALU.mult, [bones], [bones64])
    ts("dve", omka[:], ppt[:, 27:31], -1.0, ALU.mult, [ppt], [omka], 1.0, ALU.add)
    ts("dve", omu[:], ppt[:, 0:15], -1.0, ALU.mult, [ppt], [omu], 1.0, ALU.add)
    act(esink[:], esink[:], AF.Exp, [esink], [esink])
    memset("dve", rmask_p[:], 1.0, [rmask_p])
    memset("dve", rmask_p[:].rearrange("p (c t) -> p c t", t=64)[:, :, 0:1], 0.0, [rmask_p])
    memset("dve", rmask_s[:], 1.0, [rmask_s])
    memset("dve", rmask_s[:].rearrange("p (c t) -> p c t", t=TS)[:, :, 0:1], 0.0, [rmask_s])
    MU, W0, A0, KK, KA, RK, LG, LB, QG, KG = 0, 15, 19, 23, 27, 31, 35, 39, 43, 44

    x1 = sb("x1", [128, 4, D])
    x1r = [Res("x1_%d" % i) for i in range(4)]
    ssq = sb("ssq", [128, 1]); rstd1 = sb("rstd1", [128, 1])
    hb = sb("hb", [128, D], BF16)
    hT = sb("hT", [128, 8, GT], BF16); h2T = sb("h2T", [128, 8, ST], BF16)
    ropec = sb("ropec", [64, GT]); ropes = sb("ropes", [64, GT])
    qT = sb("qT", [64, 8, GT], BF16); kTb = sb("kTb", [64, 2, 128 + GT], BF16); kf = sb("kf", [64, 2, GT])
    Vb = sb("Vb", [128, 3, 128], BF16); vf = sb("vf", [128, 128])
    kTs = sb("kTs", [64, 2, 128], BF16); Vs = sb("Vs", [128, 128], BF16); Vn = sb("Vn", [16, 128], BF16)
    ckf = sb("ckf", [128, 128]); kof = sb("kof", [128, 128])

    PTA = [sb("PTA%d" % i, [128, 4, 128], BF16) for i in range(1)]
    PTB = [sb("PTB%d" % i, [128, 4, 128], BF16) for i in range(1)]
    PSA = sb("PSA", [128, 4, 16], BF16); PSB = sb("PSB", [16, 4, 16], BF16)

    attT = sb("attT", [128, 4, GT], BF16); mixT = sb("mixT", [128, 4, GT], BF16)
    prw = sb("prw", [128, 15, GT]); prwr = [Res("prw%d" % i) for i in range(15)]
    dsh = sb("dsh", [128, 5, GT])
    dshs = []
    for i_ in range(5):
        b_ = Buf(None, "dsh%d" % i_); b_.t = dsh.t[:, i_, :]; dshs.append(b_)
    yTp, msqp, varp, ycp, tmp2 = dshs
    dsh_all = [b_.r for b_ in dshs]
    carryS = sb("carryS", [128, 4, 15]); cmS = sb("cmS", [128, 4, 15]); rawl = sb("rawl", [128, 4, 15])
    lin = sb("lin", [128, 3, GT], BF16)
    sig = sb("sig", [128, GT]); av = sb("av", [128, GT]); gvd = [sb("gv%d" % i, [128, GT]) for i in range(2)]; gv = gvd[0]
    kk = sb("kk", [128, GT]); kkb = sb("kkb", [128, GT], BF16)
    kmd = [sb("km%d" % i, [128, GT]) for i in range(2)]; km = kmd[0]; bb = sb("bb", [128, GT]); tmp = sb("tmp", [128, GT])
    cum = sb("cum", [128, GT]); Wt = sb("Wt", [128, GT]); Wi = sb("Wi", [128, GT])
    rn = cum; Wp = sig
    wcsd = [sb("wcs%d" % i, [128, 4]) for i in range(2)]

    ATRd = [sb("ATR%d" % i, [128, 4, 256], BF16) for i in range(2)]
    BTbd = [sb("BTb%d" % i, [128, 4, 128], BF16) for i in range(2)]; KTbd = [sb("KTb%d" % i, [128, 4, 128], BF16) for i in range(2)]
    F3d = [sb("F3%d" % i, [128, 4, 3, 128], BF16) for i in range(2)]
    T3 = sb("T3", [128, 4, 3, 128], BF16)
    LTR = sb("LTR", [128, 4, 256], BF16)
    AKR = sb("AKR", [128, 4, 256], BF16)
    XX = [sb("XX%d" % i, [128, 4, 256], BF16) for i in range(2)]
    MT = [sb("MT%d" % i, [128, 4, 128], BF16) for i in range(2)]
    Zb = sb("Zb", [128, 128], BF16); Ub = sb("Ub", [128, 128], BF16)
    Tf = [sb("Tf%d" % c, [128, 128]) for c in range(4)]
    Tb = [[sb("Tb%d_%d" % (c, i), [128, 128], BF16) for i in range(2)] for c in range(4)]
    tpar = [0, 0, 0, 0]
    Sld = sb("Sld", [128, 64]); Sbd = sb("Sbd", [128, 128]); Sout = sb("Sout", [128, 128])
    ybfp = sb("ybfp", [128, GT], BF16); y2bf = sb("y2bf", [128, GT], BF16)
    rkr = sb("rkr", [128, GT], BF16)
    sqb2 = sb("sqb2", [64, 2, GT], BF16)
    relu_t = sb("relu_t", [128, 512]); den2 = relu_t; rec = dsh; relu_m = sb("relu_m", [128, GT]); uT = [sb("uT%d" % i, [128, 2, GT], BF16) for i in range(2)]
    wupr = [sb("wupr%d" % i, [128, 8, 256], BF16) for i in range(2)]
    wdnr = [sb("wdnr%d" % i, [128, 2, D], BF16) for i in range(2)]

    subres = {id(prw): prwr, id(x1): x1r, id(dsh): dsh_all}
    for i_, b_ in enumerate(allbufs[allbufs.index(x1):]):
        if b_ not in wupr and b_ not in wdnr:
            memset("pool" if i_ % 2 == 0 else "dve", b_[:], 0.0, [b_] + list(b_.cr) + list(subres.get(id(b_), [])))

    def rmsnorm_T(xrow_ap, xres, rows, gbc, dstT, col0):
        act(hb[0:rows, :], xrow_ap, AF.Square, [xres], [hb, ssq], accum_out=ssq[0:rows, :])
        act(rstd1[0:rows, :], ssq[0:rows, :], AF.Ln, [ssq], [rstd1], scale=1.0 / D, bias=1e-6)
        act(rstd1[0:rows, :], rstd1[0:rows, :], AF.Exp, [rstd1], [rstd1], scale=-0.5)
        ts("dve", hb[0:rows, :], xrow_ap, rstd1[0:rows, 0:1], ALU.mult, [xres, rstd1], [hb])
        pb = PB()
        pbv = pb.t.bitcast(BF16)
        for half in range(2):
            if half == 1:
                pb = PB(); pbv = pb.t.bitcast(BF16)
            for c4 in range(4):
                c = half * 4 + c4
                tr(pbv[:, c4 * 128:c4 * 128 + rows], hb[0:rows, c * 128:(c + 1) * 128], identb[0:rows, 0:rows], [hb, identb], [pb])
            tt("dve", dstT[:, half * 4:half * 4 + 4, col0:col0 + rows], pbv[:].rearrange("p (c t) -> p c t", t=128)[:, :, 0:rows],
               gbc[:, half * 4:half * 4 + 4].rearrange("p (c o) -> p c o", o=1).broadcast_to([128, 4, rows]), ALU.mult, [pb, gbc], [dstT])

    def attn_core(SA, SB, qap, nq, VA, VB, kA_n, kB_n, regA, regB, PTa, PTb, kvh, ocol):
        N = 4 * nq
        if SA is not None:
            for (p0, p1, q0, q1) in regA:
                act(PTa[p0:p1, :, q0:q1], SA[p0:p1, 0:N].rearrange("p (g q) -> p g q", g=4)[:, :, q0:q1], AF.Exp, [SA], [PTa], scale=0.125)
        for (p0, p1, q0, q1) in regB:
            act(PTb[p0:p1, :, q0:q1], SB[p0:p1, 0:N].rearrange("p (g q) -> p g q", g=4)[:, :, q0:q1], AF.Exp, [SB], [PTb], scale=0.125)
        dn = PA(); ob = PA()
        if SA is not None:
            mm(dn[0:64, 0:N], onesb[0:kA_n, 0:64], PTa[0:kA_n, :, 0:nq], [onesb, PTa], [dn], True, False)
        mm(dn[0:64, 0:N], onesb[0:kB_n, 0:64], PTb[0:kB_n, :, 0:nq], [onesb, PTb], [dn], SA is None, True)
        if SA is not None:
            mm(ob[0:64, 0:N], VA[0], PTa[0:kA_n, :, 0:nq], [VA[1], PTa], [ob], True, False)
        mm(ob[0:64, 0:N], VB[0], PTb[0:kB_n, :, 0:nq], [VB[1], PTb], [ob], SA is None, True)
        tt("dve", den2[0:64, 0:N].rearrange("p (g q) -> p g q", g=4), dn[0:64, 0:N].rearrange("p (g q) -> p g q", g=4),
           esink[:, kvh * 4:kvh * 4 + 4].rearrange("p (g o) -> p g o", o=1).broadcast_to([64, 4, nq]), ALU.add, [dn, esink], [den2])
        recv = rec[:].rearrange("p c t -> p (c t)")
        act(recv[0:64, 0:N], den2[0:64, 0:N], AF.Ln, [den2], dsh_all)
        act(recv[0:64, 0:N], recv[0:64, 0:N], AF.Exp, dsh_all, dsh_all, scale=-1.0)
        o4 = ob[0:64, 0:N].rearrange("p (g q) -> p g q", g=4)
        r4 = recv[0:64, 0:N].rearrange("p (g q) -> p g q", g=4)
        for par in range(2):
            tt("dve", attT[par * 64:par * 64 + 64, kvh * 2:kvh * 2 + 2, ocol:ocol + nq], o4[:, par::2, :], r4[:, par::2, :], ALU.mult, [ob] + dsh_all, [attT])

    def qk_pair(ps, T, gcol, is_k, hp):
        v3 = lambda ap: ap.rearrange("p (j t) -> p j t", j=2)
        p3 = v3(ps[0:64, 0:2 * T])
        rs3 = dsh.t[0:64, 0:2, 0:T]; qn3 = dsh.t[0:64, 2:4, 0:T]
        t13 = relu_t.t[0:64, 0:2 * GT].rearrange("p (j t) -> p j t", j=2)[:, :, 0:T]
        sq3 = sqb2.t[0:64, :, 0:T]
        cos3 = ropec[:, 0:T].rearrange("p (o t) -> p o t", o=1).broadcast_to([64, 2, T])
        sin3 = ropes[:, 0:T].rearrange("p (o t) -> p o t", o=1).broadcast_to([64, 2, T])
        act(sq3, p3, AF.Square, [ps], [sqb2])
        s2 = PA()
        mm(s2[0:64, 0:2 * T], onesb[0:64, 0:64], sq3, [onesb, sqb2], [s2])
        act(rs3, v3(s2[0:64, 0:2 * T]), AF.Ln, [s2], dsh_all, scale=1.0 / 64, bias=1e-6)
        act(rs3, rs3, AF.Exp, dsh_all, dsh_all, scale=-0.5)
        stt(qn3, p3, ppt[0:64, gcol:gcol + 1], rs3, ALU.mult, ALU.mult, [ps, ppt] + dsh_all, dsh_all)
        tt("pool", t13, qn3, cos3, ALU.mult, dsh_all + [ropec], [relu_t])
        tt("dve", rs3[0:32], qn3[32:64], sin3[32:64], ALU.mult, dsh_all + [ropes], dsh_all)
        tt("dve", rs3[32:64], qn3[0:32], sin3[0:32], ALU.mult, dsh_all + [ropes], dsh_all)
        if is_k:
            tt("dve", kf[:, 0:2, 0:T], t13, rs3, ALU.add, [relu_t] + dsh_all, [kf])
            cp("act", kTb[:, 0:2, 128:128 + T], kf[:, 0:2, 0:T], [kf], [kTb])
        else:
            tt("dve", qT[:, 2 * hp:2 * hp + 2, 0:T], t13, rs3, ALU.add, [relu_t] + dsh_all, [qT])

    def stage_A(tiles):
        col = 0
        for (xap, rows, ti) in tiles:
            ld("sp", x1[0:rows, ti, :], xap, [x1r[ti]])
            rmsnorm_T(x1[0:rows, ti, :], x1r[ti], rows, g1bc, hT, col)
            col += rows

    def mixer_group(kind, T, tiles, segs, pos0, h2col0, seq_first, seq_last, seqidx, do_A=True, before_E=None):
        C = 64 if kind == "p" else TS
        nch = T // C
        ld("sp", ropec[:, 0:T], ropecd[:, pos0:pos0 + T], [ropec]); ld("sp", ropes[:, 0:T], ropesd[:, pos0:pos0 + T], [ropes])
        if do_A:
            stage_A(tiles)
        stage(1)
        nseg = len(segs)
        sn0 = segs[0][2]
        for (si, scol, sn, sfirst, slast, sidx) in segs:
            if kind == "s":
                ld("sp", carryS[:, si, :], sshift[sidx], [carryS])
            elif sfirst:
                memset("pool", carryS[:, 0, :], 0.0, [carryS])
        for c in range(15):
            M = 128 if c < 14 else 32
            ps = PA()
            c0 = 768 + c * 128
            for kc in range(8):
                mm(ps[0:M, 0:T], winb[:, kc, c0:c0 + M], hT[:, kc, 0:T], [winb, hT], [ps], kc == 0, kc == 7)
            act(prw[0:M, c, 0:T], ps[0:M, 0:T], AF.Copy, [ps, omu], [prwr[c]], scale=omu[0:M, c:c + 1])
            for (si, scol, sn, sfirst, slast, sidx) in segs:
                stt(prw[0:M, c, scol + 1:scol + sn], ps[0:M, scol:scol + sn - 1], ppt[0:M, MU + c:MU + c + 1], prw[0:M, c, scol + 1:scol + sn],
                    ALU.mult, ALU.add, [ps, ppt, prwr[c]], [prwr[c]])
            act(rawl[0:M, 0:nseg, c], ps[0:M, sn0 - 1:T:sn0], AF.Copy, [ps], [rawl])
        allprw = list(prwr)
        tt("pool", cmS[:, 0:nseg, :], carryS[:, 0:nseg, :], ppt[:, MU:MU + 15].rearrange("p (o c) -> p o c", o=1).broadcast_to([128, nseg, 15]), ALU.mult, [carryS, ppt], [cmS])
        for (si, scol, sn, sfirst, slast, sidx) in segs:
            tt("pool", prw[:, :, scol:scol + 1], prw[:, :, scol:scol + 1], cmS[:, si, :].rearrange("p (c o) -> p c o", o=1), ALU.add, allprw + [cmS], allprw)
            if kind == "s":
                stq(sshifto[sidx], rawl[:, si, :], [rawl])
            elif slast and 'c' not in SKIP:
                stq(pshift[sidx], rawl[:, 0, :], [rawl])
        if kind == "p":
            cp("pool", carryS[:, 0, :], rawl[:, 0, :], [rawl], [carryS])
        act(lin[0:64, 0, 0:T], prw[0:64, 12, 0:T], AF.Tanh, allprw, [lin])
        act(lin[64:128, 0, 0:T], prw[64:128, 12, 0:T], AF.Copy, allprw, [lin])
        act(lin[:, 1, 0:T], prw[:, 13, 0:T], AF.Sigmoid, allprw, [lin])
        act(lin[0:32, 2, 0:T], prw[0:32, 14, 0:T], AF.Sigmoid, allprw, [lin])
        rmask = rmask_p if kind == "p" else rmask_s

        def bd4(ap2d, Ct):
            return ap2d.rearrange("p (c o t) -> p c o t", c=nch, o=1).broadcast_to([128, nch, 2, Ct])

        def bdo(buf, lo, Ct):
            return buf[:, 0:nch, lo:lo + 128].rearrange("p c (h t) -> p c h t", h=2)[:, :, :, 0:Ct]

        m4 = bdm[:].rearrange("p (o h) t -> p o h t", o=1)[:, :, :, 0:C].broadcast_to([128, nch, 2, C])
        def prep(c):
            r_ = prw[:, c, 0:T]; k_ = prw[:, 4 + c, 0:T]; v_ = prw[:, 8 + c, 0:T]
            cs = slice(c * 128, (c + 1) * 128)
            ATR = ATRd[c % 2]; BTb = BTbd[c % 2]; KTb = KTbd[c % 2]; F3 = F3d[c % 2]; wcs = wcsd[c % 2]; gv = gvd[c % 2]; km = kmd[c % 2]
            wc4 = wcs[:, 0:nch].rearrange("p (c o t) -> p c o t", o=1, t=1).broadcast_to([128, nch, 2, C])
            pw = PA()
            mm(pw[:, 0:T], lwb[0:64, 0, cs], lin[0:64, 0, 0:T], [lwb, lin], [pw])
            act(sig[:, 0:T], pw[:, 0:T], AF.Sigmoid, [pw, ppt], [sig], bias=ppt[:, W0 + c:W0 + c + 1])
            pa_ = PA()
            mm(pa_[:, 0:T], lwb[64:128, 0, cs], lin[64:128, 0, 0:T], [lwb, lin], [pa_])
            act(av[:, 0:T], pa_[:, 0:T], AF.Sigmoid, [pa_, ppt], [av], bias=ppt[:, A0 + c:A0 + c + 1])
            pg = PA()
            mm(pg[:, 0:T], lwb[:, 1, cs], lin[:, 1, 0:T], [lwb, lin], [pg], True, False)
            mm(pg[:, 0:T], lwb[0:32, 2, cs], lin[0:32, 2, 0:T], [lwb, lin], [pg], False, True)
            cp("act", gv[:, 0:T], pg[:, 0:T], [pg], [gv])
            ts("pool", kk[:, 0:T], k_, ppt[:, KK + c:KK + c + 1], ALU.mult, allprw + [ppt], [kk])
            tt("pool", kkb[:, 0:T], kk[:, 0:T], kk[:, 0:T], ALU.mult, [kk], [kkb])
            pn = PA()
            mm(pn[:, 0:T], bones[:], kkb[:, 0:T], [bones, kkb], [pn])
            act(rn[:, 0:T], pn[:, 0:T], AF.Ln, [pn], [rn], bias=1e-24)
            act(rn[:, 0:T], rn[:, 0:T], AF.Exp, [rn], [rn], scale=-0.5)
            tt("dve", kk[:, 0:T], kk[:, 0:T], rn[:, 0:T], ALU.mult, [kk, rn], [kk])
            ts("dve", tmp[:, 0:T], av[:, 0:T], ppt[:, KA + c:KA + c + 1], ALU.mult, [av, ppt, omka], [tmp], omka[:, c:c + 1], ALU.add)
            tt("dve", km[:, 0:T], k_, tmp[:, 0:T], ALU.mult, allprw + [tmp], [km])
            tt("pool", bb[:, 0:T], kk[:, 0:T], av[:, 0:T], ALU.mult, [kk, av], [bb])
            ts("dve", tmp[:, 0:T], sig[:, 0:T], -0.6065306597126334, ALU.mult, [sig], [tmp])
            P.op("dve", lambda e: e.tensor_tensor_scan(out=cum[:, 0:T], data0=rmask[:, 0:T], data1=tmp[:, 0:T], initial=0.0, op0=ALU.mult, op1=ALU.add), [rmask, tmp], [cum])
            act(Wt[:, 0:T], cum[:, 0:T], AF.Exp, [cum], [Wt])
            act(Wi[:, 0:T], cum[:, 0:T], AF.Exp, [cum], [Wi], scale=-1.0)
            tt("dve", tmp[:, 0:T], cum[:, 0:T], tmp[:, 0:T], ALU.subtract, [cum, tmp], [tmp])
            act(Wp[:, 0:T], tmp[:, 0:T], AF.Exp, [tmp], [Wp])
            cp("dve", wcs[:, 0:nch], Wt[:, 0:T].rearrange("p (c t) -> p c t", t=C)[:, :, C - 1], [Wt], [wcs])
            stt(tmp[:, 0:T], kk[:, 0:T], -1.0, Wp[:, 0:T], ALU.mult, ALU.mult, [kk, Wp], [tmp])
            tt("pool", bdo(ATR, 0, C), bd4(tmp[:, 0:T], C), m4, ALU.mult, [tmp, bdm], [ATR])
            tt("dve", tmp[:, 0:T], r_, Wt[:, 0:T], ALU.mult, allprw + [Wt], [tmp])
            tt("pool", bdo(ATR, 128, C), bd4(tmp[:, 0:T], C), m4, ALU.mult, [tmp, bdm], [ATR])
            tt("dve", bb[:, 0:T], bb[:, 0:T], Wi[:, 0:T], ALU.mult, [bb, Wi], [bb])
            tt("pool", bdo(BTb, 0, C), bd4(bb[:, 0:T], C), m4, ALU.mult, [bb, bdm], [BTb])
            tt("dve", tmp[:, 0:T], km[:, 0:T], Wi[:, 0:T], ALU.mult, [km, Wi], [tmp])
            tt("pool", bdo(KTb, 0, C), bd4(tmp[:, 0:T], C), m4, ALU.mult, [tmp, bdm], [KTb])
            for j, src in ((0, None), (1, BTb), (2, KTb)):
                outv = F3[:, 0:nch, j, :].rearrange("p c (h t) -> p c h t", h=2)[:, :, :, 0:C]
                if src is None:
                    tt("pool", outv, bd4(v_, C), m4, ALU.mult, allprw + [bdm], [F3])
                else:
                    tt("pool", outv, bdo(src, 0, C), wc4, ALU.mult, [src, wcs], [F3])

        def _qkatt():
            for hp in range(5):
                ps = PA()
                for j in range(2):
                    c0 = (hp * 2 + j) * 64
                    for kc in range(8):
                        mm(ps[0:64, j * T:(j + 1) * T], winb[:, kc, c0:c0 + 64], hT[:, kc, 0:T], [winb, hT], [ps], kc == 0, kc == 7)
                qk_pair(ps, T, KG if hp == 4 else QG, hp == 4, hp)
            if kind == "p":
                col = 0
                for bi, (xap, rows, ti) in enumerate(tiles):
                    ps = PB()
                    for kc in range(8):
                        mm(ps[0:rows, 0:128], hT[:, kc, col:col + rows], winb[:, kc, 640:768], [hT, winb], [ps], kc == 0, kc == 7)
                    cp("dve", Vb[0:rows, 1 + bi, :], ps[0:rows, 0:128], [ps], [Vb])
                    if seq_last and bi == len(tiles) - 1 and 'a' not in SKIP:
                        cp("act", vf[0:rows, :], ps[0:rows, 0:128], [ps], [vf])
                        stq(pv[seqidx], vf[:, :], [vf])
                    col += rows
            stage(2)
            if kind == "p":
                for cpi in range(T // 128):
                    for kvh in range(2):
                        hasA = not (seq_first and cpi == 0)
                        qap = qT[:, kvh * 4:kvh * 4 + 4, cpi * 128:cpi * 128 + 128]
                        SA = None
                        if hasA:
                            SA = PA()
                            mm(SA[:, 0:512], kTb[:, kvh, cpi * 128:cpi * 128 + 128], qap, [kTb, qT], [SA])
                        SBb = PA()
                        mm(SBb[:, 0:512], kTb[:, kvh, 128 + cpi * 128:128 + cpi * 128 + 128], qap, [kTb, qT], [SBb])
                        pi = 0
                        attn_core(SA, SBb, qap, 128,
                                  (Vb[:, cpi, kvh * 64:kvh * 64 + 64], Vb), (Vb[:, cpi + 1, kvh * 64:kvh * 64 + 64], Vb), 128, 128,
                                  [(0, 128, 0, 64), (64, 128, 64, 128)], [(0, 64, 0, 64), (0, 128, 64, 128)],
                                  PTA[pi], PTB[pi], kvh, cpi * 128)
                cp("pool", kTb[:, :, 0:128], kTb[:, :, 128 + T - 128:128 + T], [kTb], [kTb])
                cp("pool", Vb[:, 0, :], Vb[:, T // 128, :], [Vb], [Vb])
                if seq_last and 'b' not in SKIP:
                    po = PB()
                    for kvh in range(2):
                        tr(po[:, kvh * 64:kvh * 64 + 64], kf[:, kvh, T - 128:T], identf[0:64, 0:64], [kf, identf], [po])
                    cp("act", kof[:, :], po[:, 0:128], [po], [kof])
                    stq(pk[seqidx], kof[:, :], [kof])
            else:
                for s in range(NS):
                    c0 = s * TS
                    ld("sp", ckf[:], ck[s], [ckf])
                    ld("pool", Vs[:], cv[s], [Vs])
                    po = PB()
                    for kvh in range(2):
                        tr(po[0:64, kvh * 128:kvh * 128 + 128], ckf[:, kvh * 64:kvh * 64 + 64], identf[:], [ckf, identf], [po])
                    cp("dve", kTs[:], po[0:64, 0:256].rearrange("p (k t) -> p k t", k=2), [po], [kTs])
                    ps = PB()
                    for kc in range(8):
                        mm(ps[0:TS, 0:128], hT[:, kc, c0:c0 + TS], winb[:, kc, 640:768], [hT, winb], [ps], kc == 0, kc == 7)
                    cp("dve", Vn[0:TS, :], ps[0:TS, 0:128], [ps], [Vn])
                    cp("act", vf[0:TS, :], ps[0:TS, 0:128], [ps], [vf])
                    stq(sv[s], vf[0:TS, :], [vf])
                    po2 = PB()
                    for kvh in range(2):
                        tr(po2[0:TS, kvh * 64:kvh * 64 + 64], kf[:, kvh, c0:c0 + TS], identf[0:64, 0:64], [kf, identf], [po2])
                    cp("act", kof[0:TS, :], po2[0:TS, 0:128], [po2], [kof])
                    stq(sk[s], kof[0:TS, :], [kof])
                    for kvh in range(2):
                        qap = qT[:, kvh * 4:kvh * 4 + 4, c0:c0 + TS]
                        SA = PA()
                        mm(SA[:, 0:4 * TS], kTs[:, kvh, :], qap, [kTs, qT], [SA])
                        SBb = PA()
                        mm(SBb[0:TS, 0:4 * TS], kTb[:, kvh, 128 + c0:128 + c0 + TS], qap, [kTb, qT], [SBb])
                        attn_core(SA, SBb, qap, TS, (Vs[:, kvh * 64:kvh * 64 + 64], Vs), (Vn[0:TS, kvh * 64:kvh * 64 + 64], Vn), 128, TS,
                                  [(0, 128, 0, TS)], [(0, TS, 0, TS)], PSA, PSB, kvh, c0)
            stage(3)

        rr["cur"] = "side"
        side0 = P.record(lambda: prep(0))
        rr["cur"] = "main"
        main0 = P.record(_qkatt)
        rr["cur"] = "all"
        P.replay_merged(main0, side0)
        def chain(c):
            r_ = prw[:, c, 0:T]; k_ = prw[:, 4 + c, 0:T]; v_ = prw[:, 8 + c, 0:T]
            cs = slice(c * 128, (c + 1) * 128)
            ATR = ATRd[c % 2]; BTb = BTbd[c % 2]; KTb = KTbd[c % 2]; F3 = F3d[c % 2]; wcs = wcsd[c % 2]; gv = gvd[c % 2]; km = kmd[c % 2]
            wc4 = wcs[:, 0:nch].rearrange("p (c o t) -> p c o t", o=1, t=1).broadcast_to([128, nch, 2, C])
            m2 = masks[:, 0:2, :].rearrange("p a t -> p (a t)")
            pts = []
            for ch in range(nch):
                pt = PB(); ptv = pt.t.bitcast(BF16)
                for j in range(3):
                    tr(ptv[:, j * 128:(j + 1) * 128], F3[:, ch, j, :], identb[:], [F3, identb], [pt])
                pts.append((pt, ptv))
            for ch in range(nch):
                pt, ptv = pts[ch]
                cp("act", T3[:, ch, :, :], ptv[:, 0:384].rearrange("p (j t) -> p j t", j=3), [pt], [T3.c(ch)])
            p1s = []
            for ch in range(nch):
                p1 = PB()
                mm(p1[:, 0:256], BTb[:, ch, :], ATR[:, ch, :], [BTb, ATR], [p1])
                p1s.append(p1)
            for ch in range(nch):
                tt("dve", LTR[:, ch, :], p1s[ch][:, 0:256], m2, ALU.mult, [p1s[ch], masks], [LTR.c(ch)])
            p2s = []
            for ch in range(nch):
                p2 = PB()
                mm(p2[:, 0:256], KTb[:, ch, :], ATR[:, ch, :], [KTb, ATR], [p2])
                p2s.append(p2)
            for ch in range(nch):
                tt("dve", AKR[:, ch, :], p2s[ch][:, 0:256], m2, ALU.mult, [p2s[ch], masks], [AKR.c(ch)])
            p3s = []
            for ch in range(nch):
                p3 = PB()
                mm(p3[:, 0:128], ATR[:, ch, 0:128], BTb[:, ch, :], [ATR, BTb], [p3])
                p3s.append(p3)
            for ch in range(nch):
                tt("dve", XX[0][:, ch, 0:128], p3s[ch][:, 0:128], masks[:, 2, :], ALU.mult, [p3s[ch], masks], [XX[0].c(ch)])
                cp("pool", XX[0][:, ch, 128:256], LTR[:, ch, 0:128], [LTR.c(ch)], [XX[0].c(ch)])
                tt("pool", MT[0][:, ch, :], LTR[:, ch, 0:128], identb[:], ALU.add, [LTR.c(ch), identb], [MT[0].c(ch)])
            nlev = 5 if C == 64 else 3
            for lev in range(1, nlev + 1):
                a_, b_ = XX[(lev - 1) % 2], XX[lev % 2]
                ma, mb = MT[(lev - 1) % 2], MT[lev % 2]
                pxs = []
                for ch in range(nch):
                    px = PB()
                    mm(px[:, 0:128], a_[:, ch, 128:256], a_[:, ch, 0:128], [a_.c(ch)], [px])
                    if lev < nlev:
                        mm(px[:, 128:256], a_[:, ch, 0:128], a_[:, ch, 128:256], [a_.c(ch)], [px])
                    pxs.append(px)
                for ch in range(nch):
                    ee = "act" if ch % 2 == 0 else "dve"
                    if lev < nlev:
                        cp(ee, b_[:, ch, :], pxs[ch][:, 0:256], [pxs[ch]], [b_.c(ch)])
                    else:
                        cp(ee, b_[:, ch, 0:128], pxs[ch][:, 0:128], [pxs[ch]], [b_.c(ch)])
                pms = []
                for ch in range(nch):
                    pm = PB()
                    mm(pm[:, 0:128], identb[:], ma[:, ch, :], [identb, ma.c(ch)], [pm], True, False)
                    mm(pm[:, 0:128], b_[:, ch, 0:128], ma[:, ch, :], [b_.c(ch), ma.c(ch)], [pm], False, True)
                    pms.append(pm)
                for ch in range(nch):
                    cp("act" if ch % 2 == 0 else "dve", mb[:, ch, :], pms[ch][:, 0:128], [pms[ch]], [mb.c(ch)])
            MTf = MT[nlev % 2]
            for ch in range(nch):
                if kind == "s":
                    sidx = ch
                    ld("sp", Sld[:], swkv[sidx, c], [Sld])
                    tt("pool", Sbd[:].rearrange("p (h t) -> p h t", h=2), Sld[:].rearrange("p (o t) -> p o t", o=1).broadcast_to([128, 2, 64]), bdm[:], ALU.mult, [Sld, bdm], [Sbd])
                    pS = PB()
                    tr(pS[:, 0:128], Sbd[:], identf[:], [Sbd, identf], [pS])
                    cp("dve", Tf[c][:], pS[:, 0:128], [pS], [Tf[c]])
                    cp("act", Tb[c][tpar[c]][:], pS[:, 0:128], [pS], [Tb[c][tpar[c]]])
                elif seq_first and ch == 0:
                    memset("pool", Tf[c][:], 0.0, [Tf[c]])
                    memset("pool", Tb[c][tpar[c]][:], 0.0, [Tb[c][tpar[c]]])
                Told = Tb[c][tpar[c]]; Tnew = Tb[c][1 - tpar[c]]
                pG = PB()
                mm(pG[:, 0:128], ATR[:, ch, 0:128], Told[:], [ATR, Told], [pG], True, False)
                mm(pG[:, 0:128], AKR[:, ch, 0:128], T3[:, ch, 0, :], [AKR.c(ch), T3.c(ch)], [pG], False, True)
                cp("dve", Zb[:], pG[:, 0:128], [pG], [Zb])
                pU = PB()
                mm(pU[:, 0:128], MTf[:, ch, :], Zb[:], [MTf.c(ch), Zb], [pU])
                cp("act", Ub[:], pU[:, 0:128], [pU], [Ub])
                pT = PB()
                mm(pT[:, 0:128], T3[:, ch, 1, :], Ub[:], [T3.c(ch), Ub], [pT], True, False)
                mm(pT[:, 0:128], T3[:, ch, 2, :], T3[:, ch, 0, :], [T3.c(ch)], [pT], False, True)
                pY = PB()
                mm(pY[:, 0:128], Told[:], ATR[:, ch, 128:256], [Told, ATR], [pY], True, False)
                mm(pY[:, 0:128], Ub[:], LTR[:, ch, 128:256], [Ub, LTR.c(ch)], [pY], False, False)
                mm(pY[:, 0:128], T3[:, ch, 0, :], AKR[:, ch, 128:256], [T3.c(ch), AKR.c(ch)], [pY], False, True)
                stt(Tnew[:], Tf[c][:], wcs[:, ch:ch + 1], pT[:, 0:128], ALU.mult, ALU.add, [Tf[c], wcs, pT], [Tnew])
                stt(Tf[c][:], Tf[c][:], wcs[:, ch:ch + 1], pT[:, 0:128], ALU.mult, ALU.add, [Tf[c], wcs, pT], [Tf[c]])
                tpar[c] = 1 - tpar[c]
                cp("dve", yTp[0:64, ch * C:ch * C + C], pY[0:64, 0:C], [pY], [yTp])
                cp("act", yTp[64:128, ch * C:ch * C + C], pY[64:128, 64:64 + C], [pY], [yTp])
                if (kind == "s" or (seq_last and ch == nch - 1)) and 'd' not in SKIP:
                    pS = PB()
                    tr(pS[:, 0:128], Tf[c][:], identf[:], [Tf[c], identf], [pS])
                    cp("dve", Sout[:], pS[:, 0:128], [pS], [Sout])
                    dst = swkvo if kind == "s" else pwkv
                    di = ch if kind == "s" else seqidx
                    stq(dst[di, 2 * c], Sout[0:64, 0:64], [Sout])
                    stq(dst[di, 2 * c + 1], Sout[64:128, 64:128], [Sout])

        def post(c):
            r_ = prw[:, c, 0:T]; k_ = prw[:, 4 + c, 0:T]; v_ = prw[:, 8 + c, 0:T]
            cs = slice(c * 128, (c + 1) * 128)
            ATR = ATRd[c % 2]; BTb = BTbd[c % 2]; KTb = KTbd[c % 2]; F3 = F3d[c % 2]; wcs = wcsd[c % 2]; gv = gvd[c % 2]; km = kmd[c % 2]
            wc4 = wcs[:, 0:nch].rearrange("p (c o t) -> p c o t", o=1, t=1).broadcast_to([128, nch, 2, C])
            cp("act", ybfp[:, 0:T], yTp[:, 0:T], [yTp], [ybfp])
            act(y2bf[:, 0:T], yTp[:, 0:T], AF.Square, [yTp], [y2bf])
            pmn = PA(); pe2 = PA()
            mm(pmn[:, 0:T], bones64[:], ybfp[:, 0:T], [bones64, ybfp], [pmn])
            mm(pe2[:, 0:T], bones64[:], y2bf[:, 0:T], [bones64, y2bf], [pe2])
            act(msqp[:, 0:T], pmn[:, 0:T], AF.Square, [pmn], [msqp])
            tt("dve", varp[:, 0:T], pe2[:, 0:T], msqp[:, 0:T], ALU.subtract, [pe2, msqp], [varp])
            act(varp[:, 0:T], varp[:, 0:T], AF.Ln, [varp], [varp], bias=64e-5)
            act(varp[:, 0:T], varp[:, 0:T], AF.Exp, [varp], [varp], scale=-0.5)
            tt("dve", ycp[:, 0:T], yTp[:, 0:T], pmn[:, 0:T], ALU.subtract, [yTp, pmn], [ycp])
            tt("dve", ycp[:, 0:T], ycp[:, 0:T], varp[:, 0:T], ALU.mult, [ycp, varp], [ycp])
            ts("dve", ycp[:, 0:T], ycp[:, 0:T], ppt[:, LG + c:LG + c + 1], ALU.mult, [ycp, ppt], [ycp], ppt[:, LB + c:LB + c + 1], ALU.add)
            stt(rkr[:, 0:T], r_, ppt[:, RK + c:RK + c + 1], km[:, 0:T], ALU.mult, ALU.mult, allprw + [ppt, km], [rkr])
            pbs = PA()
            mm(pbs[:, 0:T], bones[:], rkr[:, 0:T], [bones, rkr], [pbs])
            tt("dve", tmp2[:, 0:T], pbs[:, 0:T], v_, ALU.mult, [pbs] + allprw, [tmp2])
            tt("dve", ycp[:, 0:T], ycp[:, 0:T], tmp2[:, 0:T], ALU.add, [ycp, tmp2], [ycp])
            tt("dve", mixT[:, c, 0:T], ycp[:, 0:T], gv[:, 0:T], ALU.mult, [ycp, gv], [mixT])

        for c in range(4):
            def _main(c=c):
                chain(c)
                post(c)
            if c < 3:
                rr["cur"] = "side"
                side = P.record(lambda c=c: prep(c + 1))
                rr["cur"] = "main"
                main = P.record(_main)
                rr["cur"] = "all"
                P.replay_merged(main, side)
            else:
                _main()
        stage(4)
        if before_E is not None:
            P.mark()
            before_E()
        col = 0
        for (xap, rows, ti) in tiles:
            for half in range(2):
                po = PA()
                hs = slice(half * 512, half * 512 + 512)
                for kc in range(4):
                    mm(po[0:rows, :], attT[:, kc, col:col + rows], woutb[:, kc, hs], [attT, woutb], [po], kc == 0, False)
                for kc in range(4):
                    mm(po[0:rows, :], mixT[:, kc, col:col + rows], woutb[:, 4 + kc, hs], [mixT, woutb], [po], False, kc == 3)
                tt("dve", x1[0:rows, ti, hs], x1[0:rows, ti, hs], po[0:rows, :], ALU.add, [x1r[ti], po], [x1r[ti]])
            rmsnorm_T(x1[0:rows, ti, :], x1r[ti], rows, g2bc, h2T, h2col0 + col)
            col += rows

    def mlp(T, tiles, outs, hcol0):
        for sp2 in range(NSL // 2):
            for j in range(2):
                s = 2 * sp2 + j
                wu = wupr[j]; wd = wdnr[j]
                ld("sp", wu[:], wups[s], [wu], [wsu[s]]); ld("sp", wd[:], wdns[s], [wd], [wsd[s]])
                u = uT[j]
                for m in range(2):
                    pu = PA()
                    for kc in range(8):
                        mm(pu[:, 0:T], wu[:, kc, m * 128:(m + 1) * 128], h2T[:, kc, hcol0:hcol0 + T], [wu, h2T], [pu], kc == 0, kc == 7)
                    act(relu_m[:, 0:T], pu[:, 0:T], AF.Relu, [pu], [relu_m])
                    tt("pool", u[:, m, 0:T], relu_m[:, 0:T], relu_m[:, 0:T], ALU.mult, [relu_m], [u])
            for (rows, ti, c0) in tiles:
                for half in range(2):
                    pd = PA()
                    hs = slice(half * 512, half * 512 + 512)
                    for j in range(2):
                        for m in range(2):
                            mm(pd[0:rows, :], uT[j][:, m, c0:c0 + rows], wdnr[j][:, m, hs], [uT[j], wdnr[j]], [pd], j == 0 and m == 0, j == 1 and m == 1)
                    tt("dve", x1[0:rows, ti, hs], x1[0:rows, ti, hs], pd[0:rows, :], ALU.add, [x1r[ti], pd], [x1r[ti]])
        for (rows, ti, c0), oap in zip(tiles, outs):
            stq(oap, x1[0:rows, ti, :], [x1r[ti]])

    jobs = []
    for sq_ in range(NSEQ):
        ng = SEQ // GT
        for g in range(ng):
            r0 = sq_ * SEQ + g * GT
            par = len(jobs) % 2
            tiles = [(xp[r0 + i * 128:r0 + (i + 1) * 128, :], 128, par * 2 + i) for i in range(GT // 128)]
            segs = [(0, 0, GT, g == 0, g == ng - 1, sq_)]
            mix = (lambda do_A, before_E, tiles=tiles, segs=segs, g=g, par=par, ng=ng, sq_=sq_:
                   mixer_group("p", GT, tiles, segs, g * GT, par * GT, g == 0, g == ng - 1, sq_, do_A, before_E))
            ffn = (lambda r0=r0, par=par:
                   mlp(GT, [(128, par * 2 + i, i * 128) for i in range(2)], [yp[r0 + i * 128:r0 + (i + 1) * 128, :] for i in range(2)], par * GT))
            jobs.append((mix, ffn, (lambda tiles=tiles: stage_A(tiles))))
    if NS > 0:
        Ts = NS * TS
        pars = len(jobs) % 2
        tiles_s = [(xs[0:Ts, :], Ts, pars * 2)]

        def mix_s(do_A, before_E):
            for b_ in ATRd + BTbd + KTbd + F3d:
                memset("pool", b_[:], 0.0, [b_])
            mixer_group("s", Ts, tiles_s, [(s, s * TS, TS, True, True, s) for s in range(NS)], SEQ, pars * GT, True, True, 0, do_A, before_E)
        jobs.append((mix_s, lambda: mlp(Ts, [(Ts, pars * 2, 0)], [ys[0:Ts, :]], pars * GT), lambda: stage_A(tiles_s)))
    prev_ffn = None
    for i, (mix, ffn, afn) in enumerate(jobs):
        nxtA = jobs[i + 1][2] if i + 1 < len(jobs) else None
        if prev_ffn is None:
            rr["cur"] = "all"
            main = P.record(lambda: mix(True, nxtA))
            P.replay_merged(main, ffn_convert)
        else:
            rr["cur"] = "mlp"
            side = P.record(prev_ffn)
            rr["cur"] = "all"
            main = P.record(lambda: mix(False, nxtA))
            P.replay_merged(main, side)
        prev_ffn = ffn
    rr["cur"] = "mlp"
    prev_ffn()
    P.finish()
    P.emit(st)
    st.close()
    return nc


def _feat(v, n):
    return np.ascontiguousarray(v.reshape(n, 128).T)


def _feat15(v):
    out = np.zeros((128, 15), np.float32)
    out[:, :14] = v[:1792].reshape(14, 128).T
    out[:32, 14] = v[1792:1824]
    return out


def _unfeat15(a):
    return np.concatenate([a[:, :14].T.reshape(-1), a[:32, 14]])


def _consts(SEQ, TS, NS):
    ident = np.eye(128, dtype=np.float32)
    rperm = np.zeros((64, 64), np.float32)
    for m in range(32):
        rperm[m + 32, m] = -1.0
        rperm[m, m + 32] = 1.0
    idx = np.arange(128)
    same = (idx[:, None] // 64) == (idx[None, :] // 64)
    r, c = idx[:, None] % 64, idx[None, :] % 64
    masks = np.stack([same & (r < c), same & (r <= c), same & (r > c)], 1).astype(np.float32)
    half = 32
    inv = (10000.0 ** (-np.arange(half, dtype=np.float32) / half)).astype(np.float32)
    pos = np.concatenate([np.arange(SEQ)] + [PAST + np.arange(TS)] * NS).astype(np.float32)
    ang = pos[None, :] * inv[:, None]
    cos = np.cos(ang).astype(np.float32); sin = np.sin(ang).astype(np.float32)
    return ident, rperm, masks, np.concatenate([cos, cos], 0), np.concatenate([sin, -sin], 0)


_CACHE = {}
_HOOK = None


def kernel(x_prompt, x_sample, cache_attn_k, cache_attn_v, state_rwkv_wkv, state_rwkv_shift,
           ln1_g, w_in, q_norm_g, k_norm_g, attn_sinks, shift_mu, decay_w0, decay_w2, iclr_a0, iclr_a2,
           gate_g2, k_k, k_a, r_k, lnx_g, lnx_b, w_out, ln2_g, w_up, w_down):
    f = lambda a: np.ascontiguousarray(np.asarray(a, dtype=np.float32))
    x_prompt, x_sample = f(x_prompt), f(x_sample)
    B, SEQ, _ = x_prompt.shape
    BS, TS, _ = x_sample.shape
    NSEQ, NS = B // NCORES, BS // NCORES
    key = (NSEQ, SEQ, NS, TS)
    if key not in _CACHE:
        _CACHE[key] = build(*key)
    nc = _CACHE[key]
    ident, rperm, masks, ropec, ropes = _consts(SEQ, TS, NS)
    pp = np.zeros((128, 48), np.float32)
    pp[:, 0:15] = _feat15(f(shift_mu)[0])
    for off, v in ((15, decay_w0), (19, iclr_a0), (23, k_k), (27, k_a), (31, f(r_k).reshape(1, 512)), (35, lnx_g), (39, lnx_b)):
        pp[:, off:off + 4] = _feat(f(v)[0], 4)
    pp[:64, 43] = f(q_norm_g)[0]
    pp[:64, 44] = f(k_norm_g)[0]
    common = dict(g1=_feat(f(ln1_g)[0], 8), g2=_feat(f(ln2_g)[0], 8), w_in=f(w_in)[0], w_out=f(w_out)[0], w_up=f(w_up)[0], w_down=f(w_down)[0],
                  dw2=f(decay_w2)[0], ia2=f(iclr_a2)[0], gg2=f(gate_g2)[0], pp=pp, sinks=f(attn_sinks),
                  ident=ident, rperm=rperm, masks=masks, ropec=ropec, ropes=ropes)
    ckf = f(cache_attn_k)[0].reshape(BS, 128, 128); cvf = f(cache_attn_v)[0].reshape(BS, 128, 128)
    wkv = f(state_rwkv_wkv)[0].reshape(BS, 4, 128, 64)
    shf = np.stack([_feat15(r) for r in f(state_rwkv_shift)[0]], 0)
    in_maps = []
    for c in range(NCORES):
        m = dict(common)
        m["xp"] = x_prompt[c * NSEQ:(c + 1) * NSEQ].reshape(NSEQ * SEQ, D)
        m["xs"] = x_sample[c * NS:(c + 1) * NS].reshape(NS * TS, D)
        m["ck"] = ckf[c * NS:(c + 1) * NS]; m["cv"] = cvf[c * NS:(c + 1) * NS]
        m["swkv"] = wkv[c * NS:(c + 1) * NS]; m["sshift"] = shf[c * NS:(c + 1) * NS]
        in_maps.append(m)
    if _HOOK is not None:
        R = _HOOK(nc, in_maps)
    else:
        R = run_bass_kernel_spmd(nc, in_maps, core_ids=list(range(NCORES))).results
    cat = lambda k: np.concatenate([np.asarray(r[k], dtype=np.float32) for r in R], 0)
    y_p = cat("yp").reshape(B, SEQ, D); y_s = cat("ys").reshape(BS, TS, D)
    p_k = cat("pk").reshape(1, B, 128, 2, 64); p_v = cat("pv").reshape(1, B, 128, 2, 64)
    p_wkv = cat("pwkv").reshape(1, B, 8, 64, 64)
    p_sh = np.stack([_unfeat15(a) for a in cat("pshift")], 0).reshape(1, B, RWC)
    s_k = cat("sk").reshape(1, BS, TS, 2, 64); s_v = cat("sv").reshape(1, BS, TS, 2, 64)
    s_wkv = cat("swkvo").reshape(1, BS, 8, 64, 64)
    s_sh = np.stack([_unfeat15(a) for a in cat("sshifto")], 0).reshape(1, BS, RWC)
    return (y_p, y_s, p_k, p_v, p_wkv, p_sh, s_k, s_v, s_wkv, s_sh)
```
